# Optimizing a Trainium2 kernel written in Bass

```python
import jax, jax.numpy as jnp
from jax import lax
import numpy as np

D_MODEL = 1024
BATCH = 8
SEQ = 4096
DEPTH = 1

CHUNK = 64
N_LEFT_CHUNKS = 8
BAND = (N_LEFT_CHUNKS + 1) * CHUNK
ATT_HEADS = 8
HEAD_DIM = 64
D_ATT = ATT_HEADS * HEAD_DIM
REL_CLIP = 256
N_REL = 2 * REL_CLIP + 1
SGU_BLOCK = 128
SGU_GROUPS = 8
SGU_GROUP_DIM = 64
D_SGU = SGU_GROUPS * SGU_GROUP_DIM
D_FF = 2816
D_IN = 3 * D_ATT + 2 * D_SGU + 2 * D_MODEL
EPS = 1e-6
NEG_INF = -1e30

kernel_name = "macaron_gated_chunkattn_gmlp_block"


def rmsnorm(x, g):
    xf = x.astype(jnp.float32)
    y = xf * lax.rsqrt(jnp.mean(xf * xf, axis=-1, keepdims=True) + EPS)
    return (y * g.astype(jnp.float32)).astype(x.dtype)


def layernorm(x, g, b):
    xf = x.astype(jnp.float32)
    mu = jnp.mean(xf, axis=-1, keepdims=True)
    var = jnp.mean(jnp.square(xf - mu), axis=-1, keepdims=True)
    y = (xf - mu) * lax.rsqrt(var + EPS)
    return (y * g.astype(jnp.float32) + b.astype(jnp.float32)).astype(x.dtype)


def swiglu(h, w_gate, w_up, w_down):
    return (jax.nn.silu(h @ w_gate) * (h @ w_up)) @ w_down


def chunked_rel_attention(q, k, v, rel_table):
    B, S = q.shape[0], q.shape[1]
    n_c = S // CHUNK
    pad = N_LEFT_CHUNKS * CHUNK
    qc = q.reshape(B, n_c, CHUNK, ATT_HEADS, HEAD_DIM)
    kp = jnp.pad(k, ((0, 0), (pad, 0), (0, 0), (0, 0))).reshape(B, n_c + N_LEFT_CHUNKS, CHUNK, ATT_HEADS, HEAD_DIM)
    vp = jnp.pad(v, ((0, 0), (pad, 0), (0, 0), (0, 0))).reshape(B, n_c + N_LEFT_CHUNKS, CHUNK, ATT_HEADS, HEAD_DIM)
    k_band = jnp.concatenate([kp[:, i:i + n_c] for i in range(N_LEFT_CHUNKS + 1)], axis=2)
    v_band = jnp.concatenate([vp[:, i:i + n_c] for i in range(N_LEFT_CHUNKS + 1)], axis=2)
    scores = jnp.einsum('bcqhd,bckhd->bchqk', qc, k_band).astype(jnp.float32) * (HEAD_DIM ** -0.5)
    qi = jnp.arange(CHUNK)[:, None]
    kj = jnp.arange(BAND)[None, :]
    rel = jnp.clip(qi + pad - kj, -REL_CLIP, REL_CLIP) + REL_CLIP
    bias = rel_table.astype(jnp.float32)[:, rel]
    key_pos = jnp.arange(n_c)[:, None] * CHUNK + jnp.arange(BAND)[None, :] - pad
    valid = key_pos >= 0
    scores = jnp.where(valid[None, :, None, None, :], scores + bias[None, None], NEG_INF)
    probs = jax.nn.softmax(scores, axis=-1).astype(v.dtype)
    out = jnp.einsum('bchqk,bckhd->bcqhd', probs, v_band)
    return out.reshape(B, S, D_ATT)


def spatial_gating(z, ln_g, ln_b, w_s, b_s):
    B, S = z.shape[0], z.shape[1]
    n_blk = S // SGU_BLOCK
    u, vs = z[..., :D_SGU], z[..., D_SGU:]
    vs = layernorm(vs, ln_g, ln_b).reshape(B, n_blk, SGU_BLOCK, SGU_GROUPS, SGU_GROUP_DIM)
    pos = jnp.arange(SGU_BLOCK)
    mask = (pos[:, None] // CHUNK) >= (pos[None, :] // CHUNK)
    w_m = jnp.where(mask[None], w_s, jnp.zeros_like(w_s))
    s = jnp.einsum('gij,bnjgd->bnigd', w_m, vs) + b_s.T[None, None, :, :, None]
    return u * s.reshape(B, S, D_SGU)


def setup_inputs(seed: int = 0) -> dict:
    key = jax.random.key(seed)
    ks = jax.random.split(key, 24)
    L = DEPTH
    f32 = jnp.float32

    def nrm(k, shape, scale):
        return jax.random.normal(k, shape, f32) * scale

    def gain(k, shape):
        return 1.0 + 0.05 * jax.random.normal(k, shape, f32)

    return {
        "x": jax.random.normal(ks[0], (BATCH, SEQ, D_MODEL), f32),
        "norm_ffn1": gain(ks[1], (L, D_MODEL)),
        "ffn1_w_gate": nrm(ks[2], (L, D_MODEL, D_FF), D_MODEL ** -0.5),
        "ffn1_w_up": nrm(ks[3], (L, D_MODEL, D_FF), D_MODEL ** -0.5),
        "ffn1_w_down": nrm(ks[4], (L, D_FF, D_MODEL), D_FF ** -0.5),
        "norm_mix": gain(ks[5], (L, D_MODEL)),
        "w_in": nrm(ks[6], (L, D_MODEL, D_IN), D_MODEL ** -0.5),
        "b_gate": nrm(ks[7], (L, 2 * D_MODEL), 0.01),
        "rel_bias": nrm(ks[8], (L, ATT_HEADS, N_REL), 0.1),
        "sgu_ln_g": gain(ks[9], (L, D_SGU)),
        "sgu_ln_b": nrm(ks[10], (L, D_SGU), 0.01),
        "sgu_w_s": nrm(ks[11], (L, SGU_GROUPS, SGU_BLOCK, SGU_BLOCK), SGU_BLOCK ** -0.5),
        "sgu_b_s": gain(ks[12], (L, SGU_GROUPS, SGU_BLOCK)),
        "w_branch_att": nrm(ks[13], (L, D_ATT, D_MODEL), D_ATT ** -0.5),
        "w_branch_sgu": nrm(ks[14], (L, D_SGU, D_MODEL), D_SGU ** -0.5),
        "w_out": nrm(ks[15], (L, D_MODEL, D_MODEL), D_MODEL ** -0.5),
        "norm_ffn2": gain(ks[16], (L, D_MODEL)),
        "ffn2_w_gate": nrm(ks[17], (L, D_MODEL, D_FF), D_MODEL ** -0.5),
        "ffn2_w_up": nrm(ks[18], (L, D_MODEL, D_FF), D_MODEL ** -0.5),
        "ffn2_w_down": nrm(ks[19], (L, D_FF, D_MODEL), D_FF ** -0.5),
        "norm_final": gain(ks[20], (D_MODEL,)),
    }


def reference(x, norm_ffn1, ffn1_w_gate, ffn1_w_up, ffn1_w_down, norm_mix, w_in, b_gate,
              rel_bias, sgu_ln_g, sgu_ln_b, sgu_w_s, sgu_b_s, w_branch_att, w_branch_sgu,
              w_out, norm_ffn2, ffn2_w_gate, ffn2_w_up, ffn2_w_down, norm_final):
    B, S = x.shape[0], x.shape[1]
    for l in range(DEPTH):
        x = x + 0.5 * swiglu(rmsnorm(x, norm_ffn1[l]), ffn1_w_gate[l], ffn1_w_up[l], ffn1_w_down[l])
        h = rmsnorm(x, norm_mix[l])
        z = h @ w_in[l]
        o = 0
        q = z[..., o:o + D_ATT].reshape(B, S, ATT_HEADS, HEAD_DIM); o += D_ATT
        k = z[..., o:o + D_ATT].reshape(B, S, ATT_HEADS, HEAD_DIM); o += D_ATT
        v = z[..., o:o + D_ATT].reshape(B, S, ATT_HEADS, HEAD_DIM); o += D_ATT
        z_sgu = jax.nn.gelu(z[..., o:o + 2 * D_SGU]); o += 2 * D_SGU
        g = jax.nn.sigmoid(z[..., o:o + 2 * D_MODEL] + b_gate[l])
        g_att, g_sgu = g[..., :D_MODEL], g[..., D_MODEL:]
        y_att = chunked_rel_attention(q, k, v, rel_bias[l])
        y_sgu = spatial_gating(z_sgu, sgu_ln_g[l], sgu_ln_b[l], sgu_w_s[l], sgu_b_s[l])
        merged = g_att * (y_att @ w_branch_att[l]) + g_sgu * (y_sgu @ w_branch_sgu[l])
        x = x + merged @ w_out[l]
        x = x + 0.5 * swiglu(rmsnorm(x, norm_ffn2[l]), ffn2_w_gate[l], ffn2_w_up[l], ffn2_w_down[l])
    return rmsnorm(x, norm_final)
```

```python
import numpy as np
import concourse.bass as bass
import concourse.mybir as mybir
from concourse.bass_utils import run_bass_kernel_spmd
from contextlib import ExitStack

F32 = mybir.dt.float32
BF16 = mybir.dt.bfloat16
AF = mybir.ActivationFunctionType
ALU = mybir.AluOpType
AX = mybir.AxisListType

D = 1024
S_LEN = 4096
DFF = 2816
NFC = DFF // 128
T = 512
NT = S_LEN // T
NSLOT = 6
EPS = 1e-6


class _Op:
    __slots__ = ("eng", "fn", "deps", "is_dma", "semkey", "count", "signal", "idx", "label")

    def __init__(self, eng, fn, is_dma=False, semkey=None):
        self.eng = eng
        self.fn = fn
        self.deps = []
        self.is_dma = is_dma
        self.semkey = semkey
        self.count = None
        self.signal = False
        self.idx = -1


class Prog:
    ENGS = ("pe", "act", "dve", "pool", "sp")

    def __init__(self, nc):
        self.nc = nc
        self.ops = {e: [] for e in self.ENGS}
        self.last_writer = {}
        self.readers = {}
        self.dma_counts = {}
        self.label = ""

    def _track(self, op, reads, writes):
        deps = []
        raw = set()
        for r in reads:
            w = self.last_writer.get(r)
            if w is not None:
                deps.append(w)
                raw.add(id(w))
        for w_ in writes:
            w = self.last_writer.get(w_)
            if w is not None:
                deps.append(w)
            deps.extend(self.readers.get(w_, ()))
        best = {}
        seen = set()
        for d in deps:
            if id(d) in seen or d is op:
                continue
            seen.add(id(d))
            if d.is_dma:
                op.deps.append(d)
            elif d.eng != op.eng or op.eng in ("act", "dve", "pool"):
                b = best.get(d.eng)
                if b is None or d.idx > b.idx:
                    best[d.eng] = d
        for d in best.values():
            op.deps.append(d)
            d.signal = True
        for r in reads:
            self.readers.setdefault(r, []).append(op)
        for w_ in writes:
            self.last_writer[w_] = op
            self.readers[w_] = []

    def add(self, eng, fn, reads=(), writes=()):
        op = _Op(eng, fn)
        op.idx = len(self.ops[eng])
        op.label = self.label
        self._track(op, reads, writes)
        self.ops[eng].append(op)
        return op

    def dma(self, queue, fn, reads=(), writes=(), sem=None):
        op = _Op(queue, fn, is_dma=True, semkey=("dma", sem))
        op.idx = len(self.ops[queue])
        op.label = self.label
        self._track(op, reads, writes)
        c = self.dma_counts.get(sem, 0) + 16
        self.dma_counts[sem] = c
        op.count = c
        self.ops[queue].append(op)
        return op

    def emit(self):
        nc = self.nc
        for e in self.ENGS:
            c = 0
            for op in self.ops[e]:
                if not op.is_dma:
                    op.semkey = ("eng", e)
                    if op.signal:
                        c += 1
                        op.count = c
        semkeys = [("eng", e) for e in self.ENGS] + [("dma", k) for k in self.dma_counts]
        with ExitStack() as st:
            sems = {k: st.enter_context(nc.semaphore("s_" + "_".join(str(x) for x in k))) for k in semkeys}
            block = st.enter_context(nc.Block())
            engmap = {"pe": block.tensor, "act": block.scalar, "dve": block.vector,
                      "pool": block.gpsimd, "sp": block.sync}

            def make(e):
                def body(eng):
                    waited = {}
                    for op in self.ops[e]:
                        for d in op.deps:
                            if waited.get(d.semkey, 0) >= d.count:
                                continue
                            eng.wait_ge(sems[d.semkey], d.count)
                            waited[d.semkey] = d.count
                        inst = op.fn(eng)
                        if op.is_dma:
                            inst.then_inc(sems[op.semkey], 16)
                        elif op.signal:
                            inst.then_inc(sems[op.semkey], 1)
                return body

            for e in self.ENGS:
                if self.ops[e]:
                    engmap[e](make(e))


def build_nc(n_tiles=NT, debug=False):
    nc = bass.Bass("TRN2", target_bir_lowering=False)

    def din(name, shape):
        return nc.dram_tensor(name, list(shape), F32, kind="ExternalInput").ap()

    x_d = din("x", [S_LEN, D])
    fsrc_d = [din("fsrc1", [NFC, 128, 2048]), din("fsrc2", [NFC, 128, 2048])]
    wd_d = [din("wd1", [NFC, 128, 1024]), din("wd2", [NFC, 128, 1024])]
    wisrc_d = din("wisrc", [5, 128, 4096])
    mgsrc_d = din("mgsrc", [8, 128, 3072])
    wosrc_d = din("wosrc", [2, 128, 4096])
    g3_d = din("g3", [128, 24])
    gfin_d = din("gfin", [D])
    bgate_d = din("bgate", [128, 16])
    relb_d = din("relb", [4, 128, 576])
    lng_d = din("lng", [512])
    lnb_d = din("lnb", [512])
    wst_d = din("wst", [128, 1024])
    bs_d = din("bs", [1, 1024])
    ident_d = din("ident", [128, 128])
    out_d = nc.dram_tensor("out", [S_LEN, D], F32, kind="ExternalOutput").ap()
    dbg_d = nc.dram_tensor("dbg", [128, 16, T], F32, kind="ExternalOutput").ap() if debug else None

    ffs_d = [nc.dram_tensor(f"ffs{i}", [NFC, 128, 2048], BF16, kind="Internal").ap() for i in range(2)]
    wds_d = [nc.dram_tensor(f"wds{i}", [NFC, 128, 1024], BF16, kind="Internal").ap() for i in range(2)]
    wis_d = nc.dram_tensor("wis", [5, 128, 4096], BF16, kind="Internal").ap()
    mgs_d = nc.dram_tensor("mgs", [8, 128, 3072], BF16, kind="Internal").ap()
    wos_d = nc.dram_tensor("wos", [2, 128, 4096], BF16, kind="Internal").ap()

    with ExitStack() as st:
        def sb(name, shape, dt):
            return st.enter_context(nc.sbuf_tensor(name, list(shape), dt))

        PS = [st.enter_context(nc.psum_tensor(f"ps{i}", [128, 1024], F32)) for i in range(4)]

        def bank(i):
            return PS[i // 2][:, (i % 2) * 512:(i % 2 + 1) * 512]

        def rb(i):
            return ("ps", i)

        X = [sb(f"X{i}", [128, 4, D], F32) for i in range(2)]
        XN = [sb(f"XN{i}", [128, D], BF16) for i in range(4)]
        XT = sb("XT", [128, 8, T], BF16)
        HT = sb("HT", [128, NFC, T], BF16)
        SG = [sb(f"SG{i}", [128, T], F32) for i in range(2)]
        SLOT = [sb(f"SLOT{i}", [128, 4096], BF16) for i in range(NSLOT)]
        QBD = sb("QBD", [128, 4, 8, 128], BF16)
        KT = sb("KT", [128, 4, 2 * T], BF16)
        VR = sb("VR", [128, 8, 512], BF16)
        UT = sb("UT", [128, 4, T], F32)
        RB = sb("RB", [128, 4, 576], BF16)
        PB = [[sb(f"PB{a}{b}", [128, 640], BF16) for b in range(2)] for a in range(2)]
        PTS = [sb(f"PTS{i}", [128, 640], BF16) for i in range(2)]
        DG = [sb(f"DG{i}", [128, 128], BF16) for i in range(4)]
        STAT = sb("STAT", [128, 64], F32)
        ASTAT = sb("ASTAT", [128, 16], F32)
        VG = [sb(f"VG{i}", [128, 512], F32) for i in range(2)]
        VSN = [sb(f"VSN{i}", [128, 512], BF16) for i in range(4)]
        BNS = sb("BNS", [128, 32], F32)
        GATE = [sb(f"GATE{i}", [128, T], F32) for i in range(2)]
        M1 = [sb(f"M1{i}", [128, T], F32) for i in range(2)]
        OB = [sb(f"OB{i}", [128, D], F32) for i in range(2)]
        IDB = sb("IDB", [128, 128], BF16)
        G3 = sb("G3", [128, 24], F32)
        GF = sb("GF", [128, D], F32)
        BG = sb("BG", [128, 16], F32)
        LNG = sb("LNG", [128, 512], F32)
        LNB = sb("LNB", [128, 512], F32)
        WST = sb("WST", [128, 1024], BF16)
        BSH = sb("BSH", [1, 1024], BF16)
        BSF = OB[0][0:1, :]
        BSHF = OB[1][0:1, :]
        BSL = sb("BSL", [1, 1024], BF16)
        ONES = sb("ONES", [1, 128], BF16)
        EPST = sb("EPST", [128, 1], F32)

        p = Prog(nc)

        p.dma("sp", lambda e: e.dma_start(out=G3[:], in_=g3_d), writes=["G3"], sem="c0")
        p.dma("sp", lambda e: e.dma_start(out=GF[:], in_=gfin_d.partition_broadcast(128)), writes=["GF"], sem="c1")
        p.dma("sp", lambda e: e.dma_start(out=BG[:], in_=bgate_d), writes=["BG"], sem="c2")
        p.dma("pool", lambda e: e.dma_start(out=RB[:], in_=relb_d.rearrange("h p k -> p h k")), writes=["RB"], sem="c3")
        p.dma("sp", lambda e: e.dma_start(out=LNG[:], in_=lng_d.partition_broadcast(128)), writes=["LNG"], sem="c4")
        p.dma("sp", lambda e: e.dma_start(out=LNB[:], in_=lnb_d.partition_broadcast(128)), writes=["LNB"], sem="c5")
        p.dma("sp", lambda e: e.dma_start(out=BSF, in_=bs_d), writes=[("OB", 0)], sem="c6")
        p.dma("pool", lambda e: e.dma_start(out=IDB[:], in_=ident_d), writes=["IDB"], sem="c7")
        p.dma("pool", lambda e: e.dma_start(out=WST[:], in_=wst_d), writes=["WST"], sem="c8")
        p.dma("pool", lambda e: e.dma_start(out=BSH[:], in_=bs_d), writes=["BSH"], sem="c9")
        p.add("dve", lambda e: e.memset(EPST[:], EPS), writes=["EPS"])
        p.add("dve", lambda e: e.memset(ONES[:], 1.0), writes=["ONES"])
        p.add("dve", lambda e: e.memset(QBD[:].rearrange("p a b c -> p (a b c)"), 0.0),
              writes=[("QBD", i) for i in range(4)])
        p.add("dve", lambda e: e.memset(VR[:].rearrange("p a b -> p (a b)"), 0.0),
              writes=[("VR", i) for i in range(8)])
        for a in range(2):
            for b in range(2):
                p.add("dve", lambda e, a=a, b=b: e.memset(PB[a][b][:], 0.0), writes=[("PB", a, b)])
        p.add("dve", lambda e: e.memset(WST[:].rearrange("p (g i) -> p g i", g=8)[64:128, :, 0:64], 0.0),
              reads=["WST"], writes=["WST"])
        p.add("dve", lambda e: e.tensor_copy(out=BSHF, in_=BSH[:]), reads=["BSH"], writes=[("OB", 1)])
        p.add("dve", lambda e: e.tensor_tensor(out=BSHF, in0=BSF, in1=BSHF, op=ALU.subtract),
              reads=[("OB", 0), ("OB", 1)], writes=[("OB", 1)])
        p.add("dve", lambda e: e.tensor_copy(out=BSL[:], in_=BSHF), reads=[("OB", 1)], writes=["BSL"])

        slot_ctr = [0]

        converted = set()

        def chunk_aps(chunk):
            kind = chunk[0]
            if kind == "ffs":
                _, i, f2 = chunk
                return (ffs_d[i][2 * f2:2 * f2 + 2].rearrange("f p c -> p f c"),
                        fsrc_d[i][2 * f2:2 * f2 + 2].rearrange("f p c -> p f c"), 4096)
            if kind == "wds":
                _, i, g0, n = chunk
                return (wds_d[i][g0:g0 + n].rearrange("f p c -> p f c"),
                        wd_d[i][g0:g0 + n].rearrange("f p c -> p f c"), n * 1024)
            if kind == "wis":
                return wis_d[chunk[1]], wisrc_d[chunk[1]], 4096
            if kind == "mgs":
                return mgs_d[chunk[1]], mgsrc_d[chunk[1]], 3072
            if kind == "wos":
                return wos_d[chunk[1]], wosrc_d[chunk[1]], 4096
            raise ValueError(chunk)

        def load_slot(chunk):
            i = slot_ctr[0] % NSLOT
            slot_ctr[0] += 1
            scr, src, ncols = chunk_aps(chunk)
            dst = SLOT[i][:, 0:ncols]
            if len(scr.shape) == 3:
                dst = dst.rearrange("p (f c) -> p f c", f=scr.shape[1])
            if chunk not in converted:
                converted.add(chunk)
                p.dma("pool", lambda e: e.dma_start(out=dst, in_=src), writes=[("slot", i)], sem=f"slotq{i}")
                p.dma("sp", lambda e: e.dma_start(out=scr, in_=dst), reads=[("slot", i)], writes=[("SCR", chunk)], sem=f"wb{i}")
            else:
                p.dma("sp", lambda e: e.dma_start(out=dst, in_=scr), reads=[("SCR", chunk)], writes=[("slot", i)], sem=f"slot{i}")
            return i

        def load_x(t):
            par = t % 2
            for s in range(4):
                r0 = t * T + s * 128
                p.dma("act" if t == 0 else "sp",
                      lambda e, par=par, s=s, r0=r0: e.dma_start(out=X[par][:, s, :], in_=x_d[r0:r0 + 128, :]),
                      writes=[("X", par, s, 0), ("X", par, s, 1)], sem=(f"xq{s}" if t == 0 else f"x{par}{s}"))

        ps_tr = [0]
        stat_ctr = [0]

        def emit_norm_stats(t, which, s):
            p.label = "emit_norm_stats(" + ",".join(str(a_) for a_ in (t, which, s,)) + ")"
            par = t % 2
            b = (which * 4 + s) % 4
            c0 = ((which * 4 + s) % 8) * 3
            xres = [("X", par, s, 0), ("X", par, s, 1)]
            p.add("act", lambda e: e.activation(out=XN[b][:], in_=X[par][:, s, :], func=AF.Square,
                                                accum_out=STAT[:, c0:c0 + 1]),
                  reads=xres, writes=[("XN", b, 0), ("XN", b, 1), ("st", c0)])
            p.add("act", lambda e: e.activation(out=STAT[:, c0 + 1:c0 + 2], in_=STAT[:, c0:c0 + 1], func=AF.Sqrt,
                                                scale=1.0 / D, bias=EPST[:, 0:1]),
                  reads=[("st", c0), "EPS"], writes=[("st", c0 + 1)])
            p.add("dve", lambda e: e.reciprocal(out=STAT[:, c0 + 2:c0 + 3], in_=STAT[:, c0 + 1:c0 + 2]),
                  reads=[("st", c0 + 1)], writes=[("st", c0 + 2)])
            p.add("act", lambda e: e.activation(out=XN[b][:, 0:512], in_=X[par][:, s, 0:512], func=AF.Copy,
                                                scale=STAT[:, c0 + 2:c0 + 3]),
                  reads=[xres[0], ("st", c0 + 2)], writes=[("XN", b, 0)])
            p.add("dve", lambda e: e.tensor_scalar(out=XN[b][:, 512:1024], in0=X[par][:, s, 512:1024],
                                                   scalar1=STAT[:, c0 + 2:c0 + 3], scalar2=None, op0=ALU.mult),
                  reads=[xres[1], ("st", c0 + 2)], writes=[("XN", b, 1)])

        def emit_norm_tr(t, which, s):
            p.label = "emit_norm_tr(" + ",".join(str(a_) for a_ in (t, which, s,)) + ")"
            b = (which * 4 + s) % 4
            bi = 6 + (ps_tr[0] % 2)
            ps_tr[0] += 1
            pbf = bank(bi).bitcast(BF16)
            for kc in range(8):
                p.add("pe", lambda e, kc=kc: e.transpose(out=pbf[:, kc * 128:(kc + 1) * 128],
                                                         in_=XN[b][:, kc * 128:(kc + 1) * 128], identity=IDB[:]),
                      reads=[("XN", b, kc // 4), "IDB"], writes=[rb(bi)])
            gb = G3[:, which * 8:(which + 1) * 8].unsqueeze(2).to_broadcast([128, 8, 128])
            p.add("dve", lambda e: e.tensor_tensor(out=XT[:, :, s * 128:(s + 1) * 128],
                                                   in0=pbf.rearrange("p (k c) -> p k c", k=8), in1=gb, op=ALU.mult),
                  reads=[rb(bi), "G3"], writes=[("XT", s)])

        XT_ALL = [("XT", s) for s in range(4)]
        ffn_ps = [0]

        def emit_ffn_stage1(t, i, after_pair=None):
            p.label = "emit_ffn_stage1(" + ",".join(str(a_) for a_ in (t, i,)) + ")"
            for f2 in range(NFC // 2):
                sl = load_slot(("ffs", i, f2))
                for a in range(2):
                    fc = 2 * f2 + a
                    W = SLOT[sl][:, a * 2048:(a + 1) * 2048]
                    k2 = ffn_ps[0] % 2
                    ffn_ps[0] += 1
                    bg, bu = 0 + k2, 2 + k2
                    for kc in range(8):
                        p.add("pe", lambda e, kc=kc, W=W, bg=bg: e.matmul(
                            bank(bg), lhsT=W[:, kc * 128:(kc + 1) * 128], rhs=XT[:, kc, :], start=(kc == 0), stop=(kc == 7)),
                            reads=[("slot", sl)] + XT_ALL, writes=[rb(bg)])
                    for kc in range(8):
                        p.add("pe", lambda e, kc=kc, W=W, bu=bu: e.matmul(
                            bank(bu), lhsT=W[:, 1024 + kc * 128:1024 + (kc + 1) * 128], rhs=XT[:, kc, :],
                            start=(kc == 0), stop=(kc == 7)),
                            reads=[("slot", sl)] + XT_ALL, writes=[rb(bu)])
                    p.add("act", lambda e, k2=k2, bg=bg: e.activation(out=SG[k2][:], in_=bank(bg), func=AF.Silu),
                          reads=[rb(bg)], writes=[("SG", k2)])
                    p.add("dve", lambda e, k2=k2, bu=bu, fc=fc: e.tensor_tensor(out=HT[:, fc, :], in0=bank(bu), in1=SG[k2][:],
                                                                                op=ALU.mult),
                          reads=[rb(bu), ("SG", k2)], writes=[("HT", fc)])
                if after_pair is not None and f2 == 1:
                    after_pair()

        def emit_ffn_stage2_pair(t, i, sp):
            p.label = "emit_ffn_stage2_pair(" + ",".join(str(a_) for a_ in (t, i, sp,)) + ")"
            par = t % 2
            for g0 in range(0, NFC, 4):
                n = min(4, NFC - g0)
                sl = load_slot(("wds", i, g0, n))
                for a in range(n):
                    fc = g0 + a
                    W = SLOT[sl][:, a * 1024:(a + 1) * 1024]
                    for sloc in range(2):
                        s = 2 * sp + sloc
                        for dh in range(2):
                            by = 2 + sloc * 2 + dh
                            p.add("pe", lambda e, fc=fc, W=W, by=by, dh=dh, s=s: e.matmul(
                                bank(by), lhsT=HT[:, fc, s * 128:(s + 1) * 128], rhs=W[:, dh * 512:(dh + 1) * 512],
                                start=(fc == 0), stop=(fc == NFC - 1)),
                                reads=[("slot", sl), ("HT", fc)], writes=[rb(by)])
            for sloc in range(2):
                s = 2 * sp + sloc
                for dh in range(2):
                    by = 2 + sloc * 2 + dh
                    xs = X[par][:, s, dh * 512:(dh + 1) * 512]
                    p.add("dve", lambda e, by=by, xs=xs: e.scalar_tensor_tensor(
                        out=xs, in0=bank(by), scalar=0.5, in1=xs, op0=ALU.mult, op1=ALU.add),
                        reads=[rb(by), ("X", par, s, dh)], writes=[("X", par, s, dh)])

        y_ps = [0]

        mx_ps = [0]

        def next_bank4():
            b = mx_ps[0] % 4
            mx_ps[0] += 1
            return b

        def emit_win(t, after_q_chunk=None):
            p.label = "emit_win(" + ",".join(str(a_) for a_ in (t,)) + ")"
            par = t % 2
            half = t % 2
            for g in (0, 1, 3):
                sl = load_slot(("wis", g))
                W = SLOT[sl]
                for cc in range(4):
                    b = next_bank4()
                    for kc in range(8):
                        p.add("pe", lambda e, kc=kc, W=W, b=b, cc=cc: e.matmul(
                            bank(b), lhsT=W[:, kc * 512 + cc * 128:kc * 512 + (cc + 1) * 128], rhs=XT[:, kc, :],
                            start=(kc == 0), stop=(kc == 7)),
                            reads=[("slot", sl)] + XT_ALL, writes=[rb(b)])
                    if g == 0:
                        for h in range(2):
                            hs = slice(h * 64, (h + 1) * 64)
                            p.add("act", lambda e, b=b, cc=cc, hs=hs: e.activation(
                                out=QBD[hs, cc, :, hs], in_=bank(b)[hs, :].rearrange("p (c q) -> p c q", c=8),
                                func=AF.Copy, scale=0.125),
                                reads=[rb(b)], writes=[("QBD", cc)])
                        if after_q_chunk is not None:
                            after_q_chunk(cc)
                    elif g == 1:
                        p.add("act", lambda e, b=b, cc=cc: e.activation(out=KT[:, cc, half * T:(half + 1) * T], in_=bank(b),
                                                                        func=AF.Copy),
                              reads=[rb(b)], writes=[("KT", cc, half)])
                    else:
                        p.add("act", lambda e, b=b, cc=cc: e.activation(out=UT[:, cc, :], in_=bank(b), func=AF.Gelu_apprx_tanh),
                              reads=[rb(b)], writes=[("UT", cc)])
        def emit_win_v(t, s_list, sl):
            p.label = "emit_win_v(" + ",".join(str(a_) for a_ in (t, s_list, sl,)) + ")"
            half = t % 2
            W = SLOT[sl]
            for s in s_list:
                b = next_bank4()
                for kc in range(8):
                    p.add("pe", lambda e, kc=kc, W=W, b=b, s=s: e.matmul(
                        bank(b), lhsT=XT[:, kc, s * 128:(s + 1) * 128], rhs=W[:, kc * 512:(kc + 1) * 512],
                        start=(kc == 0), stop=(kc == 7)),
                        reads=[("slot", sl), ("XT", s)], writes=[rb(b)])
                vs_ = half * 4 + s
                p.add("dve", lambda e, b=b, vs_=vs_: e.tensor_copy(out=VR[:, vs_, :], in_=bank(b)),
                      reads=[rb(b)], writes=[("VR", vs_)])

        VG_BUF = [(VG[0][:], ("VG", 0)), (VG[1][:], ("VG", 1)), (GATE[0][:], ("GATE", 0)), (GATE[1][:], ("GATE", 1))]

        def emit_sgu_front(t, s_list, sl):
            p.label = "emit_sgu_front(" + ",".join(str(a_) for a_ in (t, s_list, sl,)) + ")"
            W = SLOT[sl]
            for s in s_list:
                b = next_bank4()
                vg, vgres = VG_BUF[s]
                for kc in range(8):
                    p.add("pe", lambda e, kc=kc, b=b, s=s: e.matmul(
                        bank(b), lhsT=XT[:, kc, s * 128:(s + 1) * 128], rhs=W[:, kc * 512:(kc + 1) * 512],
                        start=(kc == 0), stop=(kc == 7)),
                        reads=[("slot", sl), ("XT", s)], writes=[rb(b)])
                p.add("act", lambda e, b=b, vg=vg: e.activation(out=vg, in_=bank(b), func=AF.Gelu_apprx_tanh),
                      reads=[rb(b)], writes=[vgres])

        def emit_sgu_ln(t, s_list):
            for s in s_list:
                vg, vgres = VG_BUF[s]
                b0 = s * 8
                p.add("dve", lambda e, vg=vg, b0=b0: e.bn_stats(out=BNS[:, b0:b0 + 6], in_=vg), reads=[vgres], writes=[("BNS", s)])
                p.add("dve", lambda e, b0=b0: e.bn_aggr(out=BNS[:, b0 + 6:b0 + 8], in_=BNS[:, b0:b0 + 6]),
                      reads=[("BNS", s)], writes=[("BNMV", s)])
                c0 = 48 + s * 2
                p.add("act", lambda e, c0=c0, b0=b0: e.activation(out=STAT[:, c0:c0 + 1], in_=BNS[:, b0 + 7:b0 + 8], func=AF.Sqrt,
                                                                  scale=1.0, bias=EPST[:, 0:1]),
                      reads=[("BNMV", s), "EPS"], writes=[("st", c0)])
                p.add("dve", lambda e, c0=c0: e.reciprocal(out=STAT[:, c0 + 1:c0 + 2], in_=STAT[:, c0:c0 + 1]),
                      reads=[("st", c0)], writes=[("st", c0 + 1)])
                p.add("dve", lambda e, vg=vg, c0=c0, b0=b0: e.tensor_scalar(
                    out=vg, in0=vg, scalar1=BNS[:, b0 + 6:b0 + 7], scalar2=STAT[:, c0 + 1:c0 + 2],
                    op0=ALU.subtract, op1=ALU.mult),
                    reads=[vgres, ("BNMV", s), ("st", c0 + 1)], writes=[vgres])
                p.add("dve", lambda e, vg=vg: e.tensor_tensor(out=vg, in0=vg, in1=LNG[:], op=ALU.mult),
                      reads=[vgres, "LNG"], writes=[vgres])
                p.add("dve", lambda e, vg=vg, s=s: e.tensor_tensor(out=VSN[s][:], in0=vg, in1=LNB[:], op=ALU.add),
                      reads=[vgres, "LNB"], writes=[("VSN", s)])

        def emit_sgu_mix(t, s_list=(0, 1, 2, 3)):
            p.label = "emit_sgu_mix(" + ",".join(str(a_) for a_ in (t,)) + ")"
            bb = (4, 5)
            SM = PS[2]
            for s in s_list:
                for pr in range(4):
                    o = SM[:, pr * 256:(pr + 1) * 256]
                    rbk = rb(bb[pr // 2])
                    p.add("pe", lambda e, o=o, pr=pr: e.matmul(o, lhsT=ONES[0:1, :], rhs=BSH[0:1, pr * 256:(pr + 1) * 256],
                                                              start=True, stop=False),
                          reads=["ONES", "BSH"], writes=[rbk])
                    p.add("pe", lambda e, o=o, pr=pr: e.matmul(o, lhsT=ONES[0:1, :], rhs=BSL[0:1, pr * 256:(pr + 1) * 256],
                                                              start=False, stop=False),
                          reads=["ONES", "BSL"], writes=[rbk])
                    p.add("pe", lambda e, o=o, pr=pr, s=s: e.matmul(o, lhsT=VSN[s][:, pr * 128:(pr + 1) * 128],
                                                                   rhs=WST[:, pr * 256:(pr + 1) * 256],
                                                                   start=False, stop=True),
                          reads=[("VSN", s), "WST"], writes=[rbk])
                for pr in range(4):
                    rbk = rb(bb[pr // 2])
                    for h in range(2):
                        hs = slice(h * 64, (h + 1) * 64)
                        p.add("dve", lambda e, pr=pr, h=h, hs=hs, s=s: e.tensor_tensor(
                            out=HT[hs, 12 + pr, s * 128:(s + 1) * 128],
                            in0=SM[hs, pr * 256 + h * 128:pr * 256 + (h + 1) * 128],
                            in1=UT[hs, pr, s * 128:(s + 1) * 128], op=ALU.mult),
                            reads=[rbk, ("UT", pr)], writes=[("HT", 12 + pr)])

        def emit_attention(t, extra=None):
            p.label = "emit_attention(" + ",".join(str(a_) for a_ in (t,)) + ")"
            half = t % 2
            items = [(hp, cl) for hp in range(4) for cl in range(8)]
            N = len(items)

            def geom(i):
                hp, cl = items[i]
                c = t * 8 + cl
                n_prev = 0 if t == 0 else 8 - cl
                n_cur = cl + 1
                nb = (n_prev + n_cur) * 64
                return dict(hp=hp, cl=cl, n_prev=n_prev, n_cur=n_cur, j0=576 - nb, off=64 * (cl % 2),
                            gb0=(c - 8) // 2, k2=i % 2, k4=i % 4, a0=(i % 4) * 4,
                            pbuf=PB[cl % 2][(i // 2) % 2], pres=("PB", cl % 2, (i // 2) % 2))

            def stage_scores(i):
                g = geom(i)
                hp, cl, n_prev, n_cur, j0, k2, a0 = g["hp"], g["cl"], g["n_prev"], g["n_cur"], g["j0"], g["k2"], g["a0"]
                S2 = PS[k2]
                sres = [rb(2 * k2), rb(2 * k2 + 1)]
                if n_prev:
                    o = S2[:, 512 - n_prev * 64:512]
                    p.add("pe", lambda e: e.matmul(
                        o, lhsT=QBD[:, hp, cl, :],
                        rhs=KT[:, hp, (1 - half) * T + cl * 64:(1 - half) * T + T], start=True, stop=False),
                        reads=[("QBD", hp), ("KT", hp, 1 - half)], writes=[sres[0]])
                    p.add("pe", lambda e: e.matmul(o, lhsT=IDB[:], rhs=RB[:, hp, 0:n_prev * 64], start=False, stop=True),
                          reads=["IDB", "RB"], writes=[sres[0]])
                o2 = S2[:, 512:512 + n_cur * 64]
                p.add("pe", lambda e: e.matmul(
                    o2, lhsT=QBD[:, hp, cl, :],
                    rhs=KT[:, hp, half * T:half * T + n_cur * 64], start=True, stop=False),
                    reads=[("QBD", hp), ("KT", hp, half)], writes=[sres[1]])
                p.add("pe", lambda e: e.matmul(o2, lhsT=IDB[:], rhs=RB[:, hp, 576 - n_cur * 64:576], start=False, stop=True),
                      reads=["IDB", "RB"], writes=[sres[1]])
                band = S2[:, 512 - n_prev * 64:512 + n_cur * 64]
                p.add("dve", lambda e: e.reduce_max(out=ASTAT[:, a0:a0 + 1], in_=band, axis=AX.X, negate=True),
                      reads=sres, writes=[("as", a0)])

            def stage_exp(i):
                g = geom(i)
                j0, k2, a0, off, pbuf, pres, n_prev, n_cur = g["j0"], g["k2"], g["a0"], g["off"], g["pbuf"], g["pres"], g["n_prev"], g["n_cur"]
                S2 = PS[k2]
                sres = [rb(2 * k2), rb(2 * k2 + 1)]
                band = S2[:, 512 - n_prev * 64:512 + n_cur * 64]
                if t == 0 and j0 > 0:
                    p.add("dve", lambda e: e.memset(pbuf[:, off:off + j0], 0.0), writes=[pres])
                p.add("act", lambda e: e.activation(
                    out=pbuf[:, off + j0:off + 576], in_=band, func=AF.Exp,
                    bias=ASTAT[:, a0:a0 + 1], scale=1.0, accum_out=ASTAT[:, a0 + 1:a0 + 2]),
                    reads=sres + [("as", a0)], writes=[pres, ("as", a0 + 1)])

            def stage_norm(i):
                g = geom(i)
                k4, a0 = g["k4"], g["a0"]
                p.add("dve", lambda e: e.reciprocal(out=ASTAT[:, a0 + 2:a0 + 3], in_=ASTAT[:, a0 + 1:a0 + 2]),
                      reads=[("as", a0 + 1)], writes=[("as", a0 + 2)])
                p.add("act", lambda e: e.activation(out=DG[k4][:], in_=IDB[:], func=AF.Copy, scale=ASTAT[:, a0 + 2:a0 + 3]),
                      reads=["IDB", ("as", a0 + 2)], writes=[("DG", k4)])

            def stage_pt(i):
                g = geom(i)
                k2, k4, pbuf, pres = g["k2"], g["k4"], g["pbuf"], g["pres"]
                PTp = PS[2]
                for kt in range(5):
                    pc = kt * 128 if kt < 3 else 512 + (kt - 3) * 128
                    p.add("pe", lambda e, kt=kt, pc=pc: e.matmul(
                        PTp[:, pc:pc + 128], lhsT=pbuf[:, kt * 128:(kt + 1) * 128], rhs=DG[k4][:],
                        start=True, stop=True),
                        reads=[pres, ("DG", k4)], writes=[rb(4 + kt // 3)])
                p.add("act", lambda e: e.activation(out=PTS[k2][:, 0:384], in_=PTp[:, 0:384], func=AF.Copy),
                      reads=[rb(4)], writes=[("PTS", k2, 0)])
                p.add("dve", lambda e: e.tensor_copy(out=PTS[k2][:, 384:640], in_=PTp[:, 512:768]),
                      reads=[rb(5)], writes=[("PTS", k2, 1)])

            def stage_pv(i):
                g = geom(i)
                hp, cl, k2, gb0 = g["hp"], g["cl"], g["k2"], g["gb0"]
                ob = 6 + k2
                for kt in range(5):
                    vslot = (gb0 + kt) % 8
                    p.add("pe", lambda e, kt=kt, vslot=vslot: e.matmul(
                        bank(ob)[:, 0:128], lhsT=VR[:, vslot, hp * 128:(hp + 1) * 128],
                        rhs=PTS[k2][:, kt * 128:(kt + 1) * 128], start=(kt == 0), stop=(kt == 4)),
                        reads=[("VR", vslot), ("PTS", k2, kt // 3)], writes=[rb(ob)])
                for h in range(2):
                    hs = slice(h * 64, (h + 1) * 64)
                    p.add("dve", lambda e, hs=hs, h=h: e.tensor_copy(
                        out=HT[hs, 8 + hp, cl * 64:(cl + 1) * 64], in_=bank(ob)[hs, h * 64:(h + 1) * 64]),
                        reads=[rb(ob)], writes=[("HT", 8 + hp)])

            for step in range(N + 4):
                if extra is not None and step in extra:
                    extra[step]()
                    p.label = "emit_attention(" + str(t) + ")"
                if step < N:
                    stage_scores(step)
                if 0 <= step - 3 < N:
                    stage_pt(step - 3)
                if step < N:
                    stage_exp(step)
                if 0 <= step - 1 < N:
                    stage_norm(step - 1)
                if 0 <= step - 4 < N:
                    stage_pv(step - 4)

        merge_pre = {}

        def emit_merge_gates(t, dc):
            sl = load_slot(("mgs", dc))
            W = SLOT[sl]
            b0 = 4 * (dc % 2)
            for j in range(2):
                bgt = b0 + 2 * j
                for kc in range(8):
                    p.add("pe", lambda e, kc=kc, W=W, bgt=bgt, j=j: e.matmul(
                        bank(bgt), lhsT=W[:, j * 1024 + kc * 128:j * 1024 + (kc + 1) * 128], rhs=XT[:, kc, :],
                        start=(kc == 0), stop=(kc == 7)),
                        reads=[("slot", sl)] + XT_ALL, writes=[rb(bgt)])
            merge_pre[(t, dc)] = sl

        def emit_merge(t):
            p.label = "emit_merge(" + ",".join(str(a_) for a_ in (t,)) + ")"
            par = t % 2
            YA = [("HT", 8 + i) for i in range(4)]
            YS = [("HT", 12 + i) for i in range(4)]
            for dc in range(8):
                if (t, dc) not in merge_pre:
                    emit_merge_gates(t, dc)
                sl = merge_pre[(t, dc)]
                W = SLOT[sl]
                k2 = dc % 2
                b0 = 4 * k2
                for j in range(2):
                    bgt, bbr = b0 + 2 * j, b0 + 2 * j + 1
                    for c4 in range(4):
                        p.add("pe", lambda e, c4=c4, W=W, bbr=bbr, j=j: e.matmul(
                            bank(bbr), lhsT=W[:, 2048 + j * 512 + c4 * 128:2048 + j * 512 + (c4 + 1) * 128],
                            rhs=HT[:, 8 + 4 * j + c4, :], start=(c4 == 0), stop=(c4 == 3)),
                            reads=[("slot", sl)] + (YA if j == 0 else YS), writes=[rb(bbr)])
                    p.add("act", lambda e, bgt=bgt, j=j, dc=dc: e.activation(
                        out=GATE[j][:], in_=bank(bgt), func=AF.Sigmoid, bias=BG[:, j * 8 + dc:j * 8 + dc + 1], scale=1.0),
                        reads=[rb(bgt), "BG"], writes=[("GATE", j)])
                    p.add("dve", lambda e, bbr=bbr, j=j: e.tensor_tensor(out=M1[j][:], in0=bank(bbr), in1=GATE[j][:], op=ALU.mult),
                          reads=[rb(bbr), ("GATE", j)], writes=[("M1", j)])
                p.add("dve", lambda e, dc=dc: e.tensor_tensor(out=HT[:, dc, :], in0=M1[0][:], in1=M1[1][:], op=ALU.add),
                      reads=[("M1", 0), ("M1", 1)], writes=[("HT", dc)])
            sls = [load_slot(("wos", dh)) for dh in range(2)]
            for s in range(4):
                for dh in range(2):
                    W = SLOT[sls[dh]]
                    by = 4 + (y_ps[0] % 2)
                    y_ps[0] += 1
                    for dc in range(8):
                        p.add("pe", lambda e, dc=dc, W=W, by=by, s=s: e.matmul(
                            bank(by), lhsT=HT[:, dc, s * 128:(s + 1) * 128], rhs=W[:, dc * 512:(dc + 1) * 512],
                            start=(dc == 0), stop=(dc == 7)),
                            reads=[("slot", sls[dh]), ("HT", dc)], writes=[rb(by)])
                    xs = X[par][:, s, dh * 512:(dh + 1) * 512]
                    p.add("dve", lambda e, by=by, xs=xs: e.tensor_tensor(out=xs, in0=bank(by), in1=xs, op=ALU.add),
                          reads=[rb(by), ("X", par, s, dh)], writes=[("X", par, s, dh)])
                emit_norm_stats(t, 2, s)
                if s >= 2:
                    emit_norm_tr(t, 2, s - 2)
            emit_norm_tr(t, 2, 2)
            emit_norm_tr(t, 2, 3)

        pending_stores = []

        def flush_stores():
            while pending_stores:
                pending_stores.pop(0)()

        def emit_final(t, s, defer=False):
            p.label = "emit_final(" + ",".join(str(a_) for a_ in (t, s,)) + ")"
            par = t % 2
            b = s % 2
            c0 = 24 + b * 3
            xres = [("X", par, s, 0), ("X", par, s, 1)]
            p.add("act", lambda e: e.activation(out=OB[b][:], in_=X[par][:, s, :], func=AF.Square,
                                                accum_out=STAT[:, c0:c0 + 1]),
                  reads=xres, writes=[("OB", b), ("st", c0)])
            p.add("act", lambda e: e.activation(out=STAT[:, c0 + 1:c0 + 2], in_=STAT[:, c0:c0 + 1], func=AF.Sqrt,
                                                scale=1.0 / D, bias=EPST[:, 0:1]),
                  reads=[("st", c0), "EPS"], writes=[("st", c0 + 1)])
            p.add("dve", lambda e: e.reciprocal(out=STAT[:, c0 + 2:c0 + 3], in_=STAT[:, c0 + 1:c0 + 2]),
                  reads=[("st", c0 + 1)], writes=[("st", c0 + 2)])
            p.add("dve", lambda e: e.scalar_tensor_tensor(out=OB[b][:], in0=X[par][:, s, :], scalar=STAT[:, c0 + 2:c0 + 3],
                                                          in1=GF[:], op0=ALU.mult, op1=ALU.mult),
                  reads=xres + [("st", c0 + 2), "GF"], writes=[("OB", b)])
            r0 = t * T + s * 128

            def store():
                p.dma("act", lambda e: e.dma_start(out=out_d[r0:r0 + 128, :], in_=OB[b][:]),
                      reads=[("OB", b)], writes=[("outd", b)], sem=f"o{b}")
            if defer:
                pending_stores.append(store)
            else:
                store()

        load_x(0)
        for s in range(4):
            emit_norm_stats(0, 0, s)
            emit_norm_tr(0, 0, s)
        for t in range(n_tiles):
            emit_ffn_stage1(t, 0, after_pair=flush_stores)
            if t + 1 < n_tiles:
                load_x(t + 1)
            emit_ffn_stage2_pair(t, 0, 0)
            emit_norm_stats(t, 1, 0)
            emit_norm_stats(t, 1, 1)
            emit_ffn_stage2_pair(t, 0, 1)
            emit_norm_tr(t, 1, 0)
            emit_norm_tr(t, 1, 1)
            emit_norm_stats(t, 1, 2)
            emit_norm_stats(t, 1, 3)
            slv = load_slot(("wis", 2))
            slvs = load_slot(("wis", 4))
            emit_win_v(t, [0, 1], slv)
            emit_sgu_front(t, [0, 1], slvs)
            emit_norm_tr(t, 1, 2)
            emit_norm_tr(t, 1, 3)
            emit_win_v(t, [2, 3], slv)
            emit_sgu_front(t, [2, 3], slvs)
            emit_win(t, after_q_chunk=lambda cc, t=t: emit_sgu_ln(t, [cc]))
            emit_sgu_mix(t, (0, 1))
            emit_attention(t, extra={1: (lambda t=t: emit_sgu_mix(t, (2,))), 2: (lambda t=t: emit_sgu_mix(t, (3,))),
                                     33: (lambda t=t: emit_merge_gates(t, 0))})
            emit_merge(t)
            if debug and t == n_tiles - 1:
                p.dma("pool", lambda e: e.dma_start(out=dbg_d, in_=HT[:, 0:16, :]),
                      reads=[("HT", i) for i in range(16)], writes=["dbg"], sem="dbg")
            emit_ffn_stage1(t, 1)
            if t + 1 < n_tiles:
                for s in range(4):
                    emit_norm_stats(t + 1, 0, s)
            emit_ffn_stage2_pair(t, 1, 0)
            if t + 1 < n_tiles:
                for s in range(4):
                    emit_norm_tr(t + 1, 0, s)
            emit_final(t, 0)
            emit_final(t, 1)
            emit_ffn_stage2_pair(t, 1, 1)
            emit_final(t, 2, defer=(t + 1 < n_tiles))
            emit_final(t, 3, defer=(t + 1 < n_tiles))
        p.add("sp", lambda e: e.nop(), reads=[("outd", 0), ("outd", 1)] + (["dbg"] if debug else []))
        _NC_CACHE['prog'] = p
        p.emit()
        print('sbuf bytes remaining', nc.sbuf_bytes_remaining)
    return nc


_NC_CACHE = {}


def _host_inputs(inputs):
    f = lambda a: np.ascontiguousarray(np.asarray(a, dtype=np.float32))
    g3 = np.concatenate([f(inputs[k])[0].reshape(8, 128).T for k in ("norm_ffn1", "norm_mix", "norm_ffn2")], axis=1)
    rel_tab = f(inputs["rel_bias"])[0]
    qi = np.arange(64)[:, None]
    kj = np.arange(576)[None, :]
    rel = np.clip(qi + 512 - kj, -256, 256) + 256
    relb = rel_tab[:, rel].reshape(4, 128, 576)
    def chunked(w, nk, nc_, cw):
        return w.reshape(nk, 128, nc_, cw).transpose(2, 1, 0, 3).reshape(nc_, 128, nk * cw)

    def fsrc(wg, wu):
        return f(np.concatenate([chunked(f(wg)[0], 8, NFC, 128), chunked(f(wu)[0], 8, NFC, 128)], axis=2))

    w_in = f(inputs["w_in"])[0]
    mgsrc = np.concatenate([chunked(w_in[:, 2560:3584], 8, 8, 128), chunked(w_in[:, 3584:4608], 8, 8, 128),
                            chunked(f(inputs["w_branch_att"])[0], 4, 8, 128),
                            chunked(f(inputs["w_branch_sgu"])[0], 4, 8, 128)], axis=2)
    shared = {
        "fsrc1": fsrc(inputs["ffn1_w_gate"], inputs["ffn1_w_up"]),
        "fsrc2": fsrc(inputs["ffn2_w_gate"], inputs["ffn2_w_up"]),
        "wd1": f(f(inputs["ffn1_w_down"])[0].reshape(NFC, 128, 1024)),
        "wd2": f(f(inputs["ffn2_w_down"])[0].reshape(NFC, 128, 1024)),
        "wisrc": f(chunked(w_in[:, 0:2560], 8, 5, 512)),
        "mgsrc": f(mgsrc),
        "wosrc": f(chunked(f(inputs["w_out"])[0], 8, 2, 512)),
        "g3": f(g3), "gfin": f(inputs["norm_final"]),
        "bgate": f(f(inputs["b_gate"])[0].reshape(16, 128).T),
        "relb": f(relb),
        "lng": f(inputs["sgu_ln_g"])[0], "lnb": f(inputs["sgu_ln_b"])[0],
        "wst": f(f(inputs["sgu_w_s"])[0].transpose(2, 0, 1).reshape(128, 1024)),
        "bs": f(f(inputs["sgu_b_s"])[0].reshape(1, 1024)),
        "ident": np.eye(128, dtype=np.float32),
    }
    return shared


def kernel(**inputs):
    x = np.asarray(inputs["x"], dtype=np.float32)
    shared = _host_inputs(inputs)
    if "nc" not in _NC_CACHE:
        _NC_CACHE["nc"] = build_nc()
    nc = _NC_CACHE["nc"]
    in_maps = []
    for b in range(8):
        m = dict(shared)
        m["x"] = np.ascontiguousarray(x[b])
        in_maps.append(m)
    res = run_bass_kernel_spmd(nc, in_maps, core_ids=list(range(8)))
    return np.stack([np.asarray(r["out"], dtype=np.float32) for r in res.results], axis=0)
```

```python
import numpy as np
import concourse.bass as bass
import concourse.mybir as mybir
from concourse.bass_utils import run_bass_kernel_spmd
from contextlib import ExitStack

F32 = mybir.dt.float32
BF16 = mybir.dt.bfloat16
AF = mybir.ActivationFunctionType
ALU = mybir.AluOpType
AX = mybir.AxisListType

D = 1024
S_LEN = 4096
DFF = 2816
NFC = DFF // 128
T = 512
NT = S_LEN // T
NSLOT = 6
EPS = 1e-6


class _Op:
    __slots__ = ("eng", "fn", "deps", "is_dma", "semkey", "count", "signal", "idx", "label")

    def __init__(self, eng, fn, is_dma=False, semkey=None):
        self.eng = eng
        self.fn = fn
        self.deps = []
        self.is_dma = is_dma
        self.semkey = semkey
        self.count = None
        self.signal = False
        self.idx = -1


class Prog:
    ENGS = ("pe", "act", "dve", "pool", "sp")

    def __init__(self, nc):
        self.nc = nc
        self.ops = {e: [] for e in self.ENGS}
        self.last_writer = {}
        self.readers = {}
        self.dma_counts = {}
        self.label = ""

    def _track(self, op, reads, writes):
        deps = []
        raw = set()
        for r in reads:
            w = self.last_writer.get(r)
            if w is not None:
                deps.append(w)
                raw.add(id(w))
        for w_ in writes:
            w = self.last_writer.get(w_)
            if w is not None:
                deps.append(w)
            deps.extend(self.readers.get(w_, ()))
        best = {}
        seen = set()
        for d in deps:
            if id(d) in seen or d is op:
                continue
            seen.add(id(d))
            if d.is_dma:
                op.deps.append(d)
            elif d.eng != op.eng or op.eng in ("act", "dve", "pool"):
                b = best.get(d.eng)
                if b is None or d.idx > b.idx:
                    best[d.eng] = d
        for d in best.values():
            op.deps.append(d)
            d.signal = True
        for r in reads:
            self.readers.setdefault(r, []).append(op)
        for w_ in writes:
            self.last_writer[w_] = op
            self.readers[w_] = []

    def add(self, eng, fn, reads=(), writes=()):
        op = _Op(eng, fn)
        op.idx = len(self.ops[eng])
        op.label = self.label
        self._track(op, reads, writes)
        self.ops[eng].append(op)
        return op

    def dma(self, queue, fn, reads=(), writes=(), sem=None):
        op = _Op(queue, fn, is_dma=True, semkey=("dma", sem))
        op.idx = len(self.ops[queue])
        op.label = self.label
        self._track(op, reads, writes)
        c = self.dma_counts.get(sem, 0) + 16
        self.dma_counts[sem] = c
        op.count = c
        self.ops[queue].append(op)
        return op

    def emit(self):
        nc = self.nc
        for e in self.ENGS:
            c = 0
            for op in self.ops[e]:
                if not op.is_dma:
                    op.semkey = ("eng", e)
                    if op.signal:
                        c += 1
                        op.count = c
        semkeys = [("eng", e) for e in self.ENGS] + [("dma", k) for k in self.dma_counts]
        with ExitStack() as st:
            sems = {k: st.enter_context(nc.semaphore("s_" + "_".join(str(x) for x in k))) for k in semkeys}
            block = st.enter_context(nc.Block())
            engmap = {"pe": block.tensor, "act": block.scalar, "dve": block.vector,
                      "pool": block.gpsimd, "sp": block.sync}

            def make(e):
                def body(eng):
                    waited = {}
                    for op in self.ops[e]:
                        for d in op.deps:
                            if waited.get(d.semkey, 0) >= d.count:
                                continue
                            eng.wait_ge(sems[d.semkey], d.count)
                            waited[d.semkey] = d.count
                        inst = op.fn(eng)
                        if op.is_dma:
                            inst.then_inc(sems[op.semkey], 16)
                        elif op.signal:
                            inst.then_inc(sems[op.semkey], 1)
                return body

            for e in self.ENGS:
                if self.ops[e]:
                    engmap[e](make(e))


def build_nc(n_tiles=NT, debug=False):
    nc = bass.Bass("TRN2", target_bir_lowering=False)

    def din(name, shape):
        return nc.dram_tensor(name, list(shape), F32, kind="ExternalInput").ap()

    x_d = din("x", [S_LEN, D])
    fsrc_d = [din("fsrc1", [NFC, 128, 2048]), din("fsrc2", [NFC, 128, 2048])]
    wd_d = [din("wd1", [NFC, 128, 1024]), din("wd2", [NFC, 128, 1024])]
    wisrc_d = din("wisrc", [5, 128, 4096])
    mgsrc_d = din("mgsrc", [8, 128, 3072])
    wosrc_d = din("wosrc", [2, 128, 4096])
    g3_d = din("g3", [128, 24])
    gfin_d = din("gfin", [D])
    bgate_d = din("bgate", [128, 16])
    relb_d = din("relb", [4, 128, 576])
    lng_d = din("lng", [512])
    lnb_d = din("lnb", [512])
    wst_d = din("wst", [128, 1024])
    bs_d = din("bs", [1, 1024])
    ident_d = din("ident", [128, 128])
    out_d = nc.dram_tensor("out", [S_LEN, D], F32, kind="ExternalOutput").ap()
    dbg_d = nc.dram_tensor("dbg", [128, 16, T], F32, kind="ExternalOutput").ap() if debug else None

    ffs_d = [nc.dram_tensor(f"ffs{i}", [NFC, 128, 2048], BF16, kind="Internal").ap() for i in range(2)]
    wds_d = [nc.dram_tensor(f"wds{i}", [NFC, 128, 1024], BF16, kind="Internal").ap() for i in range(2)]
    wis_d = nc.dram_tensor("wis", [5, 128, 4096], BF16, kind="Internal").ap()
    mgs_d = nc.dram_tensor("mgs", [8, 128, 3072], BF16, kind="Internal").ap()
    wos_d = nc.dram_tensor("wos", [2, 128, 4096], BF16, kind="Internal").ap()

    with ExitStack() as st:
        def sb(name, shape, dt):
            return st.enter_context(nc.sbuf_tensor(name, list(shape), dt))

        PS = [st.enter_context(nc.psum_tensor(f"ps{i}", [128, 1024], F32)) for i in range(4)]

        def bank(i):
            return PS[i // 2][:, (i % 2) * 512:(i % 2 + 1) * 512]

        def rb(i):
            return ("ps", i)

        X = [sb(f"X{i}", [128, 4, D], F32) for i in range(2)]
        XN = [sb(f"XN{i}", [128, D], BF16) for i in range(4)]
        XT = sb("XT", [128, 8, T], BF16)
        HT = sb("HT", [128, NFC, T], BF16)
        SG = [sb(f"SG{i}", [128, T], F32) for i in range(2)]
        SLOT = [sb(f"SLOT{i}", [128, 4096], BF16) for i in range(NSLOT)]
        QBD = sb("QBD", [128, 4, 8, 128], BF16)
        KT = sb("KT", [128, 4, 2 * T], BF16)
        VR = sb("VR", [128, 8, 512], BF16)
        UT = sb("UT", [128, 4, T], F32)
        RB = sb("RB", [128, 4, 576], BF16)
        PB = [[sb(f"PB{a}{b}", [128, 640], BF16) for b in range(2)] for a in range(2)]
        PTS = [sb(f"PTS{i}", [128, 640], BF16) for i in range(2)]
        DG = [sb(f"DG{i}", [128, 128], BF16) for i in range(4)]
        STAT = sb("STAT", [128, 64], F32)
        ASTAT = sb("ASTAT", [128, 16], F32)
        VG = [sb(f"VG{i}", [128, 512], F32) for i in range(2)]
        VSN = [sb(f"VSN{i}", [128, 512], BF16) for i in range(4)]
        BNS = sb("BNS", [128, 32], F32)
        GATE = [sb(f"GATE{i}", [128, T], F32) for i in range(2)]
        M1 = [sb(f"M1{i}", [128, T], F32) for i in range(2)]
        OB = [sb(f"OB{i}", [128, D], F32) for i in range(2)]
        IDB = sb("IDB", [128, 128], BF16)
        G3 = sb("G3", [128, 24], F32)
        GF = sb("GF", [128, D], F32)
        BG = sb("BG", [128, 16], F32)
        LNG = sb("LNG", [128, 512], F32)
        LNB = sb("LNB", [128, 512], F32)
        WST = sb("WST", [128, 1024], BF16)
        BSH = sb("BSH", [1, 1024], BF16)
        BSF = OB[0][0:1, :]
        BSHF = OB[1][0:1, :]
        BSL = sb("BSL", [1, 1024], BF16)
        ONES = sb("ONES", [1, 128], BF16)
        EPST = sb("EPST", [128, 1], F32)

        p = Prog(nc)

        p.dma("sp", lambda e: e.dma_start(out=G3[:], in_=g3_d), writes=["G3"], sem="c0")
        p.dma("sp", lambda e: e.dma_start(out=GF[:], in_=gfin_d.partition_broadcast(128)), writes=["GF"], sem="c1")
        p.dma("sp", lambda e: e.dma_start(out=BG[:], in_=bgate_d), writes=["BG"], sem="c2")
        p.dma("pool", lambda e: e.dma_start(out=RB[:], in_=relb_d.rearrange("h p k -> p h k")), writes=["RB"], sem="c3")
        p.dma("sp", lambda e: e.dma_start(out=LNG[:], in_=lng_d.partition_broadcast(128)), writes=["LNG"], sem="c4")
        p.dma("sp", lambda e: e.dma_start(out=LNB[:], in_=lnb_d.partition_broadcast(128)), writes=["LNB"], sem="c5")
        p.dma("sp", lambda e: e.dma_start(out=BSF, in_=bs_d), writes=[("OB", 0)], sem="c6")
        p.dma("pool", lambda e: e.dma_start(out=IDB[:], in_=ident_d), writes=["IDB"], sem="c7")
        p.dma("pool", lambda e: e.dma_start(out=WST[:], in_=wst_d), writes=["WST"], sem="c8")
        p.dma("pool", lambda e: e.dma_start(out=BSH[:], in_=bs_d), writes=["BSH"], sem="c9")
        p.add("dve", lambda e: e.memset(EPST[:], EPS), writes=["EPS"])
        p.add("dve", lambda e: e.memset(ONES[:], 1.0), writes=["ONES"])
        p.add("dve", lambda e: e.memset(QBD[:].rearrange("p a b c -> p (a b c)"), 0.0),
              writes=[("QBD", i) for i in range(4)])
        p.add("dve", lambda e: e.memset(VR[:].rearrange("p a b -> p (a b)"), 0.0),
              writes=[("VR", i) for i in range(8)])
        for a in range(2):
            for b in range(2):
                p.add("dve", lambda e, a=a, b=b: e.memset(PB[a][b][:], 0.0), writes=[("PB", a, b)])
        p.add("dve", lambda e: e.memset(WST[:].rearrange("p (g i) -> p g i", g=8)[64:128, :, 0:64], 0.0),
              reads=["WST"], writes=["WST"])
        p.add("dve", lambda e: e.tensor_copy(out=BSHF, in_=BSH[:]), reads=["BSH"], writes=[("OB", 1)])
        p.add("dve", lambda e: e.tensor_tensor(out=BSHF, in0=BSF, in1=BSHF, op=ALU.subtract),
              reads=[("OB", 0), ("OB", 1)], writes=[("OB", 1)])
        p.add("dve", lambda e: e.tensor_copy(out=BSL[:], in_=BSHF), reads=[("OB", 1)], writes=["BSL"])

        slot_ctr = [0]

        converted = set()

        def chunk_aps(chunk):
            kind = chunk[0]
            if kind == "ffs":
                _, i, f2 = chunk
                return (ffs_d[i][2 * f2:2 * f2 + 2].rearrange("f p c -> p f c"),
                        fsrc_d[i][2 * f2:2 * f2 + 2].rearrange("f p c -> p f c"), 4096)
            if kind == "wds":
                _, i, g0, n = chunk
                return (wds_d[i][g0:g0 + n].rearrange("f p c -> p f c"),
                        wd_d[i][g0:g0 + n].rearrange("f p c -> p f c"), n * 1024)
            if kind == "wis":
                return wis_d[chunk[1]], wisrc_d[chunk[1]], 4096
            if kind == "mgs":
                return mgs_d[chunk[1]], mgsrc_d[chunk[1]], 3072
            if kind == "wos":
                return wos_d[chunk[1]], wosrc_d[chunk[1]], 4096
            raise ValueError(chunk)

        def load_slot(chunk):
            i = slot_ctr[0] % NSLOT
            slot_ctr[0] += 1
            scr, src, ncols = chunk_aps(chunk)
            dst = SLOT[i][:, 0:ncols]
            if len(scr.shape) == 3:
                dst = dst.rearrange("p (f c) -> p f c", f=scr.shape[1])
            if chunk not in converted:
                converted.add(chunk)
                p.dma("pool", lambda e: e.dma_start(out=dst, in_=src), writes=[("slot", i)], sem=f"slotq{i}")
                p.dma("sp", lambda e: e.dma_start(out=scr, in_=dst), reads=[("slot", i)], writes=[("SCR", chunk)], sem=f"wb{i}")
            else:
                p.dma("sp", lambda e: e.dma_start(out=dst, in_=scr), reads=[("SCR", chunk)], writes=[("slot", i)], sem=f"slot{i}")
            return i

        def load_x(t):
            par = t % 2
            for s in range(4):
                r0 = t * T + s * 128
                p.dma("act" if t == 0 else "sp",
                      lambda e, par=par, s=s, r0=r0: e.dma_start(out=X[par][:, s, :], in_=x_d[r0:r0 + 128, :]),
                      writes=[("X", par, s, 0), ("X", par, s, 1)], sem=(f"xq{s}" if t == 0 else f"x{par}{s}"))

        ps_tr = [0]
        stat_ctr = [0]

        def emit_norm_stats(t, which, s):
            p.label = "emit_norm_stats(" + ",".join(str(a_) for a_ in (t, which, s,)) + ")"
            par = t % 2
            b = (which * 4 + s) % 4
            c0 = ((which * 4 + s) % 8) * 3
            xres = [("X", par, s, 0), ("X", par, s, 1)]
            p.add("act", lambda e: e.activation(out=XN[b][:], in_=X[par][:, s, :], func=AF.Square,
                                                accum_out=STAT[:, c0:c0 + 1]),
                  reads=xres, writes=[("XN", b), ("st", c0)])
            p.add("act", lambda e: e.activation(out=STAT[:, c0 + 1:c0 + 2], in_=STAT[:, c0:c0 + 1], func=AF.Sqrt,
                                                scale=1.0 / D, bias=EPST[:, 0:1]),
                  reads=[("st", c0), "EPS"], writes=[("st", c0 + 1)])
            p.add("dve", lambda e: e.reciprocal(out=STAT[:, c0 + 2:c0 + 3], in_=STAT[:, c0 + 1:c0 + 2]),
                  reads=[("st", c0 + 1)], writes=[("st", c0 + 2)])
            p.add("act", lambda e: e.activation(out=XN[b][:], in_=X[par][:, s, :], func=AF.Copy,
                                                scale=STAT[:, c0 + 2:c0 + 3]),
                  reads=xres + [("st", c0 + 2)], writes=[("XN", b)])

        def emit_norm_tr(t, which, s):
            p.label = "emit_norm_tr(" + ",".join(str(a_) for a_ in (t, which, s,)) + ")"
            b = (which * 4 + s) % 4
            bi = 6 + (ps_tr[0] % 2)
            ps_tr[0] += 1
            pbf = bank(bi).bitcast(BF16)
            for kc in range(8):
                p.add("pe", lambda e, kc=kc: e.transpose(out=pbf[:, kc * 128:(kc + 1) * 128],
                                                         in_=XN[b][:, kc * 128:(kc + 1) * 128], identity=IDB[:]),
                      reads=[("XN", b), "IDB"], writes=[rb(bi)])
            gb = G3[:, which * 8:(which + 1) * 8].unsqueeze(2).to_broadcast([128, 8, 128])
            p.add("dve", lambda e: e.tensor_tensor(out=XT[:, :, s * 128:(s + 1) * 128],
                                                   in0=pbf.rearrange("p (k c) -> p k c", k=8), in1=gb, op=ALU.mult),
                  reads=[rb(bi), "G3"], writes=[("XT", s)])

        XT_ALL = [("XT", s) for s in range(4)]
        ffn_ps = [0]

        def emit_ffn_stage1(t, i, after_pair=None):
            p.label = "emit_ffn_stage1(" + ",".join(str(a_) for a_ in (t, i,)) + ")"
            for f2 in range(NFC // 2):
                sl = load_slot(("ffs", i, f2))
                for a in range(2):
                    fc = 2 * f2 + a
                    W = SLOT[sl][:, a * 2048:(a + 1) * 2048]
                    k2 = ffn_ps[0] % 2
                    ffn_ps[0] += 1
                    bg, bu = 0 + k2, 2 + k2
                    for kc in range(8):
                        p.add("pe", lambda e, kc=kc, W=W, bg=bg: e.matmul(
                            bank(bg), lhsT=W[:, kc * 128:(kc + 1) * 128], rhs=XT[:, kc, :], start=(kc == 0), stop=(kc == 7)),
                            reads=[("slot", sl)] + XT_ALL, writes=[rb(bg)])
                    for kc in range(8):
                        p.add("pe", lambda e, kc=kc, W=W, bu=bu: e.matmul(
                            bank(bu), lhsT=W[:, 1024 + kc * 128:1024 + (kc + 1) * 128], rhs=XT[:, kc, :],
                            start=(kc == 0), stop=(kc == 7)),
                            reads=[("slot", sl)] + XT_ALL, writes=[rb(bu)])
                    p.add("act", lambda e, k2=k2, bg=bg: e.activation(out=SG[k2][:], in_=bank(bg), func=AF.Silu),
                          reads=[rb(bg)], writes=[("SG", k2)])
                    p.add("dve", lambda e, k2=k2, bu=bu, fc=fc: e.tensor_tensor(out=HT[:, fc, :], in0=bank(bu), in1=SG[k2][:],
                                                                                op=ALU.mult),
                          reads=[rb(bu), ("SG", k2)], writes=[("HT", fc)])
                if after_pair is not None and f2 == 1:
                    after_pair()

        def emit_ffn_stage2_pair(t, i, sp):
            p.label = "emit_ffn_stage2_pair(" + ",".join(str(a_) for a_ in (t, i, sp,)) + ")"
            par = t % 2
            for g0 in range(0, NFC, 4):
                n = min(4, NFC - g0)
                sl = load_slot(("wds", i, g0, n))
                for a in range(n):
                    fc = g0 + a
                    W = SLOT[sl][:, a * 1024:(a + 1) * 1024]
                    for sloc in range(2):
                        s = 2 * sp + sloc
                        for dh in range(2):
                            by = 2 + sloc * 2 + dh
                            p.add("pe", lambda e, fc=fc, W=W, by=by, dh=dh, s=s: e.matmul(
                                bank(by), lhsT=HT[:, fc, s * 128:(s + 1) * 128], rhs=W[:, dh * 512:(dh + 1) * 512],
                                start=(fc == 0), stop=(fc == NFC - 1)),
                                reads=[("slot", sl), ("HT", fc)], writes=[rb(by)])
            for sloc in range(2):
                s = 2 * sp + sloc
                for dh in range(2):
                    by = 2 + sloc * 2 + dh
                    xs = X[par][:, s, dh * 512:(dh + 1) * 512]
                    p.add("dve", lambda e, by=by, xs=xs: e.scalar_tensor_tensor(
                        out=xs, in0=bank(by), scalar=0.5, in1=xs, op0=ALU.mult, op1=ALU.add),
                        reads=[rb(by), ("X", par, s, dh)], writes=[("X", par, s, dh)])

        y_ps = [0]

        mx_ps = [0]

        def next_bank4():
            b = mx_ps[0] % 4
            mx_ps[0] += 1
            return b

        def emit_win(t, after_q_chunk=None):
            p.label = "emit_win(" + ",".join(str(a_) for a_ in (t,)) + ")"
            par = t % 2
            half = t % 2
            for g in (0, 1, 3):
                sl = load_slot(("wis", g))
                W = SLOT[sl]
                for cc in range(4):
                    b = next_bank4()
                    for kc in range(8):
                        p.add("pe", lambda e, kc=kc, W=W, b=b, cc=cc: e.matmul(
                            bank(b), lhsT=W[:, kc * 512 + cc * 128:kc * 512 + (cc + 1) * 128], rhs=XT[:, kc, :],
                            start=(kc == 0), stop=(kc == 7)),
                            reads=[("slot", sl)] + XT_ALL, writes=[rb(b)])
                    if g == 0:
                        for h in range(2):
                            hs = slice(h * 64, (h + 1) * 64)
                            p.add("act", lambda e, b=b, cc=cc, hs=hs: e.activation(
                                out=QBD[hs, cc, :, hs], in_=bank(b)[hs, :].rearrange("p (c q) -> p c q", c=8),
                                func=AF.Copy, scale=0.125),
                                reads=[rb(b)], writes=[("QBD", cc)])
                        if after_q_chunk is not None:
                            after_q_chunk(cc)
                    elif g == 1:
                        p.add("act", lambda e, b=b, cc=cc: e.activation(out=KT[:, cc, half * T:(half + 1) * T], in_=bank(b),
                                                                        func=AF.Copy),
                              reads=[rb(b)], writes=[("KT", cc, half)])
                    else:
                        p.add("act", lambda e, b=b, cc=cc: e.activation(out=UT[:, cc, :], in_=bank(b), func=AF.Gelu_apprx_tanh),
                              reads=[rb(b)], writes=[("UT", cc)])
        def emit_win_v(t, s_list, sl):
            p.label = "emit_win_v(" + ",".join(str(a_) for a_ in (t, s_list, sl,)) + ")"
            half = t % 2
            W = SLOT[sl]
            for s in s_list:
                b = next_bank4()
                for kc in range(8):
                    p.add("pe", lambda e, kc=kc, W=W, b=b, s=s: e.matmul(
                        bank(b), lhsT=XT[:, kc, s * 128:(s + 1) * 128], rhs=W[:, kc * 512:(kc + 1) * 512],
                        start=(kc == 0), stop=(kc == 7)),
                        reads=[("slot", sl), ("XT", s)], writes=[rb(b)])
                vs_ = half * 4 + s
                p.add("dve", lambda e, b=b, vs_=vs_: e.tensor_copy(out=VR[:, vs_, :], in_=bank(b)),
                      reads=[rb(b)], writes=[("VR", vs_)])

        VG_BUF = [(VG[0][:], ("VG", 0)), (VG[1][:], ("VG", 1)), (GATE[0][:], ("GATE", 0)), (GATE[1][:], ("GATE", 1))]

        def emit_sgu_front(t, s_list, sl):
            p.label = "emit_sgu_front(" + ",".join(str(a_) for a_ in (t, s_list, sl,)) + ")"
            W = SLOT[sl]
            for s in s_list:
                b = next_bank4()
                vg, vgres = VG_BUF[s]
                for kc in range(8):
                    p.add("pe", lambda e, kc=kc, b=b, s=s: e.matmul(
                        bank(b), lhsT=XT[:, kc, s * 128:(s + 1) * 128], rhs=W[:, kc * 512:(kc + 1) * 512],
                        start=(kc == 0), stop=(kc == 7)),
                        reads=[("slot", sl), ("XT", s)], writes=[rb(b)])
                p.add("act", lambda e, b=b, vg=vg: e.activation(out=vg, in_=bank(b), func=AF.Gelu_apprx_tanh),
                      reads=[rb(b)], writes=[vgres])

        def emit_sgu_ln(t, s_list):
            for s in s_list:
                vg, vgres = VG_BUF[s]
                b0 = s * 8
                p.add("dve", lambda e, vg=vg, b0=b0: e.bn_stats(out=BNS[:, b0:b0 + 6], in_=vg), reads=[vgres], writes=[("BNS", s)])
                p.add("dve", lambda e, b0=b0: e.bn_aggr(out=BNS[:, b0 + 6:b0 + 8], in_=BNS[:, b0:b0 + 6]),
                      reads=[("BNS", s)], writes=[("BNMV", s)])
                c0 = 48 + s * 2
                p.add("act", lambda e, c0=c0, b0=b0: e.activation(out=STAT[:, c0:c0 + 1], in_=BNS[:, b0 + 7:b0 + 8], func=AF.Sqrt,
                                                                  scale=1.0, bias=EPST[:, 0:1]),
                      reads=[("BNMV", s), "EPS"], writes=[("st", c0)])
                p.add("dve", lambda e, c0=c0: e.reciprocal(out=STAT[:, c0 + 1:c0 + 2], in_=STAT[:, c0:c0 + 1]),
                      reads=[("st", c0)], writes=[("st", c0 + 1)])
                p.add("dve", lambda e, vg=vg, c0=c0, b0=b0: e.tensor_scalar(
                    out=vg, in0=vg, scalar1=BNS[:, b0 + 6:b0 + 7], scalar2=STAT[:, c0 + 1:c0 + 2],
                    op0=ALU.subtract, op1=ALU.mult),
                    reads=[vgres, ("BNMV", s), ("st", c0 + 1)], writes=[vgres])
                p.add("dve", lambda e, vg=vg: e.tensor_tensor(out=vg, in0=vg, in1=LNG[:], op=ALU.mult),
                      reads=[vgres, "LNG"], writes=[vgres])
                p.add("dve", lambda e, vg=vg, s=s: e.tensor_tensor(out=VSN[s][:], in0=vg, in1=LNB[:], op=ALU.add),
                      reads=[vgres, "LNB"], writes=[("VSN", s)])

        def emit_sgu_mix(t, s_list=(0, 1, 2, 3)):
            p.label = "emit_sgu_mix(" + ",".join(str(a_) for a_ in (t,)) + ")"
            bb = (4, 5)
            SM = PS[2]
            for s in s_list:
                for pr in range(4):
                    o = SM[:, pr * 256:(pr + 1) * 256]
                    rbk = rb(bb[pr // 2])
                    p.add("pe", lambda e, o=o, pr=pr: e.matmul(o, lhsT=ONES[0:1, :], rhs=BSH[0:1, pr * 256:(pr + 1) * 256],
                                                              start=True, stop=False),
                          reads=["ONES", "BSH"], writes=[rbk])
                    p.add("pe", lambda e, o=o, pr=pr: e.matmul(o, lhsT=ONES[0:1, :], rhs=BSL[0:1, pr * 256:(pr + 1) * 256],
                                                              start=False, stop=False),
                          reads=["ONES", "BSL"], writes=[rbk])
                    p.add("pe", lambda e, o=o, pr=pr, s=s: e.matmul(o, lhsT=VSN[s][:, pr * 128:(pr + 1) * 128],
                                                                   rhs=WST[:, pr * 256:(pr + 1) * 256],
                                                                   start=False, stop=True),
                          reads=[("VSN", s), "WST"], writes=[rbk])
                for pr in range(4):
                    rbk = rb(bb[pr // 2])
                    for h in range(2):
                        hs = slice(h * 64, (h + 1) * 64)
                        p.add("dve", lambda e, pr=pr, h=h, hs=hs, s=s: e.tensor_tensor(
                            out=HT[hs, 12 + pr, s * 128:(s + 1) * 128],
                            in0=SM[hs, pr * 256 + h * 128:pr * 256 + (h + 1) * 128],
                            in1=UT[hs, pr, s * 128:(s + 1) * 128], op=ALU.mult),
                            reads=[rbk, ("UT", pr)], writes=[("HT", 12 + pr)])

        def emit_attention(t, extra=None):
            p.label = "emit_attention(" + ",".join(str(a_) for a_ in (t,)) + ")"
            half = t % 2
            items = [(hp, cl) for hp in range(4) for cl in range(8)]
            N = len(items)

            def geom(i):
                hp, cl = items[i]
                c = t * 8 + cl
                n_prev = 0 if t == 0 else 8 - cl
                n_cur = cl + 1
                nb = (n_prev + n_cur) * 64
                return dict(hp=hp, cl=cl, n_prev=n_prev, n_cur=n_cur, j0=576 - nb, off=64 * (cl % 2),
                            gb0=(c - 8) // 2, k2=i % 2, k4=i % 4, a0=(i % 4) * 4,
                            pbuf=PB[cl % 2][(i // 2) % 2], pres=("PB", cl % 2, (i // 2) % 2))

            def stage_scores(i):
                g = geom(i)
                hp, cl, n_prev, n_cur, j0, k2, a0 = g["hp"], g["cl"], g["n_prev"], g["n_cur"], g["j0"], g["k2"], g["a0"]
                S2 = PS[k2]
                sres = [rb(2 * k2), rb(2 * k2 + 1)]
                if n_prev:
                    o = S2[:, 512 - n_prev * 64:512]
                    p.add("pe", lambda e: e.matmul(
                        o, lhsT=QBD[:, hp, cl, :],
                        rhs=KT[:, hp, (1 - half) * T + cl * 64:(1 - half) * T + T], start=True, stop=False),
                        reads=[("QBD", hp), ("KT", hp, 1 - half)], writes=[sres[0]])
                    p.add("pe", lambda e: e.matmul(o, lhsT=IDB[:], rhs=RB[:, hp, 0:n_prev * 64], start=False, stop=True),
                          reads=["IDB", "RB"], writes=[sres[0]])
                o2 = S2[:, 512:512 + n_cur * 64]
                p.add("pe", lambda e: e.matmul(
                    o2, lhsT=QBD[:, hp, cl, :],
                    rhs=KT[:, hp, half * T:half * T + n_cur * 64], start=True, stop=False),
                    reads=[("QBD", hp), ("KT", hp, half)], writes=[sres[1]])
                p.add("pe", lambda e: e.matmul(o2, lhsT=IDB[:], rhs=RB[:, hp, 576 - n_cur * 64:576], start=False, stop=True),
                      reads=["IDB", "RB"], writes=[sres[1]])
                band = S2[:, 512 - n_prev * 64:512 + n_cur * 64]
                p.add("dve", lambda e: e.reduce_max(out=ASTAT[:, a0:a0 + 1], in_=band, axis=AX.X, negate=True),
                      reads=sres, writes=[("as", a0)])

            def stage_exp(i):
                g = geom(i)
                j0, k2, a0, off, pbuf, pres, n_prev, n_cur = g["j0"], g["k2"], g["a0"], g["off"], g["pbuf"], g["pres"], g["n_prev"], g["n_cur"]
                S2 = PS[k2]
                sres = [rb(2 * k2), rb(2 * k2 + 1)]
                band = S2[:, 512 - n_prev * 64:512 + n_cur * 64]
                if t == 0 and j0 > 0:
                    p.add("dve", lambda e: e.memset(pbuf[:, off:off + j0], 0.0), writes=[pres])
                p.add("act", lambda e: e.activation(
                    out=pbuf[:, off + j0:off + 576], in_=band, func=AF.Exp,
                    bias=ASTAT[:, a0:a0 + 1], scale=1.0, accum_out=ASTAT[:, a0 + 1:a0 + 2]),
                    reads=sres + [("as", a0)], writes=[pres, ("as", a0 + 1)])

            def stage_norm(i):
                g = geom(i)
                k4, a0 = g["k4"], g["a0"]
                p.add("dve", lambda e: e.reciprocal(out=ASTAT[:, a0 + 2:a0 + 3], in_=ASTAT[:, a0 + 1:a0 + 2]),
                      reads=[("as", a0 + 1)], writes=[("as", a0 + 2)])
                p.add("act", lambda e: e.activation(out=DG[k4][:], in_=IDB[:], func=AF.Copy, scale=ASTAT[:, a0 + 2:a0 + 3]),
                      reads=["IDB", ("as", a0 + 2)], writes=[("DG", k4)])

            def stage_pt(i):
                g = geom(i)
                k2, k4, pbuf, pres = g["k2"], g["k4"], g["pbuf"], g["pres"]
                PTp = PS[2]
                for kt in range(5):
                    pc = kt * 128 if kt < 3 else 512 + (kt - 3) * 128
                    p.add("pe", lambda e, kt=kt, pc=pc: e.matmul(
                        PTp[:, pc:pc + 128], lhsT=pbuf[:, kt * 128:(kt + 1) * 128], rhs=DG[k4][:],
                        start=True, stop=True),
                        reads=[pres, ("DG", k4)], writes=[rb(4 + kt // 3)])
                p.add("act", lambda e: e.activation(out=PTS[k2][:, 0:384], in_=PTp[:, 0:384], func=AF.Copy),
                      reads=[rb(4)], writes=[("PTS", k2, 0)])
                p.add("dve", lambda e: e.tensor_copy(out=PTS[k2][:, 384:640], in_=PTp[:, 512:768]),
                      reads=[rb(5)], writes=[("PTS", k2, 1)])

            def stage_pv(i):
                g = geom(i)
                hp, cl, k2, gb0 = g["hp"], g["cl"], g["k2"], g["gb0"]
                ob = 6 + k2
                for kt in range(5):
                    vslot = (gb0 + kt) % 8
                    p.add("pe", lambda e, kt=kt, vslot=vslot: e.matmul(
                        bank(ob)[:, 0:128], lhsT=VR[:, vslot, hp * 128:(hp + 1) * 128],
                        rhs=PTS[k2][:, kt * 128:(kt + 1) * 128], start=(kt == 0), stop=(kt == 4)),
                        reads=[("VR", vslot), ("PTS", k2, kt // 3)], writes=[rb(ob)])
                for h in range(2):
                    hs = slice(h * 64, (h + 1) * 64)
                    p.add("dve", lambda e, hs=hs, h=h: e.tensor_copy(
                        out=HT[hs, 8 + hp, cl * 64:(cl + 1) * 64], in_=bank(ob)[hs, h * 64:(h + 1) * 64]),
                        reads=[rb(ob)], writes=[("HT", 8 + hp)])

            for step in range(N + 4):
                if extra is not None and step in extra:
                    extra[step]()
                    p.label = "emit_attention(" + str(t) + ")"
                if step < N:
                    stage_scores(step)
                if 0 <= step - 3 < N:
                    stage_pt(step - 3)
                if step < N:
                    stage_exp(step)
                if 0 <= step - 1 < N:
                    stage_norm(step - 1)
                if 0 <= step - 4 < N:
                    stage_pv(step - 4)

        merge_pre = {}

        def emit_merge_gates(t, dc):
            sl = load_slot(("mgs", dc))
            W = SLOT[sl]
            b0 = 4 * (dc % 2)
            for j in range(2):
                bgt = b0 + 2 * j
                for kc in range(8):
                    p.add("pe", lambda e, kc=kc, W=W, bgt=bgt, j=j: e.matmul(
                        bank(bgt), lhsT=W[:, j * 1024 + kc * 128:j * 1024 + (kc + 1) * 128], rhs=XT[:, kc, :],
                        start=(kc == 0), stop=(kc == 7)),
                        reads=[("slot", sl)] + XT_ALL, writes=[rb(bgt)])
            merge_pre[(t, dc)] = sl

        def emit_merge(t):
            p.label = "emit_merge(" + ",".join(str(a_) for a_ in (t,)) + ")"
            par = t % 2
            YA = [("HT", 8 + i) for i in range(4)]
            YS = [("HT", 12 + i) for i in range(4)]
            for dc in range(8):
                if (t, dc) not in merge_pre:
                    emit_merge_gates(t, dc)
                sl = merge_pre[(t, dc)]
                W = SLOT[sl]
                k2 = dc % 2
                b0 = 4 * k2
                for j in range(2):
                    bgt, bbr = b0 + 2 * j, b0 + 2 * j + 1
                    for c4 in range(4):
                        p.add("pe", lambda e, c4=c4, W=W, bbr=bbr, j=j: e.matmul(
                            bank(bbr), lhsT=W[:, 2048 + j * 512 + c4 * 128:2048 + j * 512 + (c4 + 1) * 128],
                            rhs=HT[:, 8 + 4 * j + c4, :], start=(c4 == 0), stop=(c4 == 3)),
                            reads=[("slot", sl)] + (YA if j == 0 else YS), writes=[rb(bbr)])
                    p.add("act", lambda e, bgt=bgt, j=j, dc=dc: e.activation(
                        out=GATE[j][:], in_=bank(bgt), func=AF.Sigmoid, bias=BG[:, j * 8 + dc:j * 8 + dc + 1], scale=1.0),
                        reads=[rb(bgt), "BG"], writes=[("GATE", j)])
                    p.add("dve", lambda e, bbr=bbr, j=j: e.tensor_tensor(out=M1[j][:], in0=bank(bbr), in1=GATE[j][:], op=ALU.mult),
                          reads=[rb(bbr), ("GATE", j)], writes=[("M1", j)])
                p.add("dve", lambda e, dc=dc: e.tensor_tensor(out=HT[:, dc, :], in0=M1[0][:], in1=M1[1][:], op=ALU.add),
                      reads=[("M1", 0), ("M1", 1)], writes=[("HT", dc)])
            sls = [load_slot(("wos", dh)) for dh in range(2)]
            for s in range(4):
                for dh in range(2):
                    W = SLOT[sls[dh]]
                    by = 4 + (y_ps[0] % 2)
                    y_ps[0] += 1
                    for dc in range(8):
                        p.add("pe", lambda e, dc=dc, W=W, by=by, s=s: e.matmul(
                            bank(by), lhsT=HT[:, dc, s * 128:(s + 1) * 128], rhs=W[:, dc * 512:(dc + 1) * 512],
                            start=(dc == 0), stop=(dc == 7)),
                            reads=[("slot", sls[dh]), ("HT", dc)], writes=[rb(by)])
                    xs = X[par][:, s, dh * 512:(dh + 1) * 512]
                    p.add("dve", lambda e, by=by, xs=xs: e.tensor_tensor(out=xs, in0=bank(by), in1=xs, op=ALU.add),
                          reads=[rb(by), ("X", par, s, dh)], writes=[("X", par, s, dh)])
                emit_norm_stats(t, 2, s)
                if s >= 2:
                    emit_norm_tr(t, 2, s - 2)
            emit_norm_tr(t, 2, 2)
            emit_norm_tr(t, 2, 3)

        pending_stores = []

        def flush_stores():
            while pending_stores:
                pending_stores.pop(0)()

        def emit_final(t, s, defer=False):
            p.label = "emit_final(" + ",".join(str(a_) for a_ in (t, s,)) + ")"
            par = t % 2
            b = s % 2
            c0 = 24 + b * 3
            xres = [("X", par, s, 0), ("X", par, s, 1)]
            p.add("act", lambda e: e.activation(out=OB[b][:], in_=X[par][:, s, :], func=AF.Square,
                                                accum_out=STAT[:, c0:c0 + 1]),
                  reads=xres, writes=[("OB", b), ("st", c0)])
            p.add("act", lambda e: e.activation(out=STAT[:, c0 + 1:c0 + 2], in_=STAT[:, c0:c0 + 1], func=AF.Sqrt,
                                                scale=1.0 / D, bias=EPST[:, 0:1]),
                  reads=[("st", c0), "EPS"], writes=[("st", c0 + 1)])
            p.add("dve", lambda e: e.reciprocal(out=STAT[:, c0 + 2:c0 + 3], in_=STAT[:, c0 + 1:c0 + 2]),
                  reads=[("st", c0 + 1)], writes=[("st", c0 + 2)])
            p.add("dve", lambda e: e.scalar_tensor_tensor(out=OB[b][:], in0=X[par][:, s, :], scalar=STAT[:, c0 + 2:c0 + 3],
                                                          in1=GF[:], op0=ALU.mult, op1=ALU.mult),
                  reads=xres + [("st", c0 + 2), "GF"], writes=[("OB", b)])
            r0 = t * T + s * 128

            def store():
                p.dma("act", lambda e: e.dma_start(out=out_d[r0:r0 + 128, :], in_=OB[b][:]),
                      reads=[("OB", b)], writes=[("outd", b)], sem=f"o{b}")
            if defer:
                pending_stores.append(store)
            else:
                store()

        load_x(0)
        for s in range(4):
            emit_norm_stats(0, 0, s)
            emit_norm_tr(0, 0, s)
        for t in range(n_tiles):
            emit_ffn_stage1(t, 0, after_pair=flush_stores)
            if t + 1 < n_tiles:
                load_x(t + 1)
            emit_ffn_stage2_pair(t, 0, 0)
            emit_norm_stats(t, 1, 0)
            emit_norm_stats(t, 1, 1)
            emit_ffn_stage2_pair(t, 0, 1)
            emit_norm_tr(t, 1, 0)
            emit_norm_tr(t, 1, 1)
            emit_norm_stats(t, 1, 2)
            emit_norm_stats(t, 1, 3)
            slv = load_slot(("wis", 2))
            slvs = load_slot(("wis", 4))
            emit_win_v(t, [0, 1], slv)
            emit_sgu_front(t, [0, 1], slvs)
            emit_norm_tr(t, 1, 2)
            emit_norm_tr(t, 1, 3)
            emit_win_v(t, [2, 3], slv)
            emit_sgu_front(t, [2, 3], slvs)
            emit_win(t, after_q_chunk=lambda cc, t=t: emit_sgu_ln(t, [cc]))
            emit_sgu_mix(t, (0, 1))
            emit_attention(t, extra={1: (lambda t=t: emit_sgu_mix(t, (2,))), 2: (lambda t=t: emit_sgu_mix(t, (3,))),
                                     33: (lambda t=t: emit_merge_gates(t, 0))})
            emit_merge(t)
            if debug and t == n_tiles - 1:
                p.dma("pool", lambda e: e.dma_start(out=dbg_d, in_=HT[:, 0:16, :]),
                      reads=[("HT", i) for i in range(16)], writes=["dbg"], sem="dbg")
            emit_ffn_stage1(t, 1)
            if t + 1 < n_tiles:
                for s in range(4):
                    emit_norm_stats(t + 1, 0, s)
            emit_ffn_stage2_pair(t, 1, 0)
            if t + 1 < n_tiles:
                for s in range(4):
                    emit_norm_tr(t + 1, 0, s)
            emit_final(t, 0)
            emit_final(t, 1)
            emit_ffn_stage2_pair(t, 1, 1)
            emit_final(t, 2, defer=(t + 1 < n_tiles))
            emit_final(t, 3, defer=(t + 1 < n_tiles))
        p.add("sp", lambda e: e.nop(), reads=[("outd", 0), ("outd", 1)] + (["dbg"] if debug else []))
        _NC_CACHE['prog'] = p
        p.emit()
        print('sbuf bytes remaining', nc.sbuf_bytes_remaining)
    return nc


_NC_CACHE = {}


def _host_inputs(inputs):
    f = lambda a: np.ascontiguousarray(np.asarray(a, dtype=np.float32))
    g3 = np.concatenate([f(inputs[k])[0].reshape(8, 128).T for k in ("norm_ffn1", "norm_mix", "norm_ffn2")], axis=1)
    rel_tab = f(inputs["rel_bias"])[0]
    qi = np.arange(64)[:, None]
    kj = np.arange(576)[None, :]
    rel = np.clip(qi + 512 - kj, -256, 256) + 256
    relb = rel_tab[:, rel].reshape(4, 128, 576)
    def chunked(w, nk, nc_, cw):
        return w.reshape(nk, 128, nc_, cw).transpose(2, 1, 0, 3).reshape(nc_, 128, nk * cw)

    def fsrc(wg, wu):
        return f(np.concatenate([chunked(f(wg)[0], 8, NFC, 128), chunked(f(wu)[0], 8, NFC, 128)], axis=2))

    w_in = f(inputs["w_in"])[0]
    mgsrc = np.concatenate([chunked(w_in[:, 2560:3584], 8, 8, 128), chunked(w_in[:, 3584:4608], 8, 8, 128),
                            chunked(f(inputs["w_branch_att"])[0], 4, 8, 128),
                            chunked(f(inputs["w_branch_sgu"])[0], 4, 8, 128)], axis=2)
    shared = {
        "fsrc1": fsrc(inputs["ffn1_w_gate"], inputs["ffn1_w_up"]),
        "fsrc2": fsrc(inputs["ffn2_w_gate"], inputs["ffn2_w_up"]),
        "wd1": f(f(inputs["ffn1_w_down"])[0].reshape(NFC, 128, 1024)),
        "wd2": f(f(inputs["ffn2_w_down"])[0].reshape(NFC, 128, 1024)),
        "wisrc": f(chunked(w_in[:, 0:2560], 8, 5, 512)),
        "mgsrc": f(mgsrc),
        "wosrc": f(chunked(f(inputs["w_out"])[0], 8, 2, 512)),
        "g3": f(g3), "gfin": f(inputs["norm_final"]),
        "bgate": f(f(inputs["b_gate"])[0].reshape(16, 128).T),
        "relb": f(relb),
        "lng": f(inputs["sgu_ln_g"])[0], "lnb": f(inputs["sgu_ln_b"])[0],
        "wst": f(f(inputs["sgu_w_s"])[0].transpose(2, 0, 1).reshape(128, 1024)),
        "bs": f(f(inputs["sgu_b_s"])[0].reshape(1, 1024)),
        "ident": np.eye(128, dtype=np.float32),
    }
    return shared


def kernel(**inputs):
    x = np.asarray(inputs["x"], dtype=np.float32)
    shared = _host_inputs(inputs)
    if "nc" not in _NC_CACHE:
        _NC_CACHE["nc"] = build_nc()
    nc = _NC_CACHE["nc"]
    in_maps = []
    for b in range(8):
        m = dict(shared)
        m["x"] = np.ascontiguousarray(x[b])
        in_maps.append(m)
    res = run_bass_kernel_spmd(nc, in_maps, core_ids=list(range(8)))
    return np.stack([np.asarray(r["out"], dtype=np.float32) for r in res.results], axis=0)
```

```python
import numpy as np
import concourse.bass as bass
import concourse.mybir as mybir
from concourse.bass_utils import run_bass_kernel_spmd
from contextlib import ExitStack

F32 = mybir.dt.float32
BF16 = mybir.dt.bfloat16
AF = mybir.ActivationFunctionType
ALU = mybir.AluOpType
AX = mybir.AxisListType

D = 1024
S_LEN = 4096
DFF = 2816
NFC = DFF // 128
T = 512
NT = S_LEN // T
NSLOT = 6
EPS = 1e-6


class _Op:
    __slots__ = ("eng", "fn", "deps", "is_dma", "semkey", "count", "signal", "idx", "label")

    def __init__(self, eng, fn, is_dma=False, semkey=None):
        self.eng = eng
        self.fn = fn
        self.deps = []
        self.is_dma = is_dma
        self.semkey = semkey
        self.count = None
        self.signal = False
        self.idx = -1


class Prog:
    ENGS = ("pe", "act", "dve", "pool", "sp")

    def __init__(self, nc):
        self.nc = nc
        self.ops = {e: [] for e in self.ENGS}
        self.last_writer = {}
        self.readers = {}
        self.dma_counts = {}
        self.label = ""

    def _track(self, op, reads, writes):
        deps = []
        raw = set()
        for r in reads:
            w = self.last_writer.get(r)
            if w is not None:
                deps.append(w)
                raw.add(id(w))
        for w_ in writes:
            w = self.last_writer.get(w_)
            if w is not None:
                deps.append(w)
            deps.extend(self.readers.get(w_, ()))
        best = {}
        seen = set()
        for d in deps:
            if id(d) in seen or d is op:
                continue
            seen.add(id(d))
            if d.is_dma:
                op.deps.append(d)
            elif d.eng != op.eng or op.eng in ("act", "dve", "pool"):
                b = best.get(d.eng)
                if b is None or d.idx > b.idx:
                    best[d.eng] = d
        for d in best.values():
            op.deps.append(d)
            d.signal = True
        for r in reads:
            self.readers.setdefault(r, []).append(op)
        for w_ in writes:
            self.last_writer[w_] = op
            self.readers[w_] = []

    def add(self, eng, fn, reads=(), writes=()):
        op = _Op(eng, fn)
        op.idx = len(self.ops[eng])
        op.label = self.label
        self._track(op, reads, writes)
        self.ops[eng].append(op)
        return op

    def dma(self, queue, fn, reads=(), writes=(), sem=None):
        op = _Op(queue, fn, is_dma=True, semkey=("dma", sem))
        op.idx = len(self.ops[queue])
        op.label = self.label
        self._track(op, reads, writes)
        c = self.dma_counts.get(sem, 0) + 16
        self.dma_counts[sem] = c
        op.count = c
        self.ops[queue].append(op)
        return op

    def emit(self):
        nc = self.nc
        for e in self.ENGS:
            c = 0
            for op in self.ops[e]:
                if not op.is_dma:
                    op.semkey = ("eng", e)
                    if op.signal:
                        c += 1
                        op.count = c
        semkeys = [("eng", e) for e in self.ENGS] + [("dma", k) for k in self.dma_counts]
        with ExitStack() as st:
            sems = {k: st.enter_context(nc.semaphore("s_" + "_".join(str(x) for x in k))) for k in semkeys}
            block = st.enter_context(nc.Block())
            engmap = {"pe": block.tensor, "act": block.scalar, "dve": block.vector,
                      "pool": block.gpsimd, "sp": block.sync}

            def make(e):
                def body(eng):
                    waited = {}
                    for op in self.ops[e]:
                        for d in op.deps:
                            if waited.get(d.semkey, 0) >= d.count:
                                continue
                            eng.wait_ge(sems[d.semkey], d.count)
                            waited[d.semkey] = d.count
                        inst = op.fn(eng)
                        if op.is_dma:
                            inst.then_inc(sems[op.semkey], 16)
                        elif op.signal:
                            inst.then_inc(sems[op.semkey], 1)
                return body

            for e in self.ENGS:
                if self.ops[e]:
                    engmap[e](make(e))


def build_nc(n_tiles=NT, debug=False):
    nc = bass.Bass("TRN2", target_bir_lowering=False)

    def din(name, shape):
        return nc.dram_tensor(name, list(shape), F32, kind="ExternalInput").ap()

    x_d = din("x", [S_LEN, D])
    fsrc_d = [din("fsrc1", [NFC, 128, 2048]), din("fsrc2", [NFC, 128, 2048])]
    wd_d = [din("wd1", [NFC, 128, 1024]), din("wd2", [NFC, 128, 1024])]
    wisrc_d = din("wisrc", [5, 128, 4096])
    mgsrc_d = din("mgsrc", [8, 128, 3072])
    wosrc_d = din("wosrc", [2, 128, 4096])
    g3_d = din("g3", [128, 24])
    gfin_d = din("gfin", [D])
    bgate_d = din("bgate", [128, 16])
    relb_d = din("relb", [4, 128, 576])
    lng_d = din("lng", [512])
    lnb_d = din("lnb", [512])
    wst_d = din("wst", [128, 1024])
    bs_d = din("bs", [1, 1024])
    ident_d = din("ident", [128, 128])
    out_d = nc.dram_tensor("out", [S_LEN, D], F32, kind="ExternalOutput").ap()
    dbg_d = nc.dram_tensor("dbg", [128, 16, T], F32, kind="ExternalOutput").ap() if debug else None

    ffs_d = [nc.dram_tensor(f"ffs{i}", [NFC, 128, 2048], BF16, kind="Internal").ap() for i in range(2)]
    wds_d = [nc.dram_tensor(f"wds{i}", [NFC, 128, 1024], BF16, kind="Internal").ap() for i in range(2)]
    wis_d = nc.dram_tensor("wis", [5, 128, 4096], BF16, kind="Internal").ap()
    mgs_d = nc.dram_tensor("mgs", [8, 128, 3072], BF16, kind="Internal").ap()
    wos_d = nc.dram_tensor("wos", [2, 128, 4096], BF16, kind="Internal").ap()

    with ExitStack() as st:
        def sb(name, shape, dt):
            return st.enter_context(nc.sbuf_tensor(name, list(shape), dt))

        PS = [st.enter_context(nc.psum_tensor(f"ps{i}", [128, 1024], F32)) for i in range(4)]

        def bank(i):
            return PS[i // 2][:, (i % 2) * 512:(i % 2 + 1) * 512]

        def rb(i):
            return ("ps", i)

        X = [sb(f"X{i}", [128, 4, D], F32) for i in range(2)]
        XN = [sb(f"XN{i}", [128, D], BF16) for i in range(4)]
        XT = sb("XT", [128, 8, T], BF16)
        HT = sb("HT", [128, NFC, T], BF16)
        SG = [sb(f"SG{i}", [128, T], F32) for i in range(2)]
        SLOT = [sb(f"SLOT{i}", [128, 4096], BF16) for i in range(NSLOT)]
        QBD = sb("QBD", [128, 4, 8, 128], BF16)
        KT = sb("KT", [128, 4, 2 * T], BF16)
        VR = sb("VR", [128, 8, 512], BF16)
        UT = sb("UT", [128, 4, T], F32)
        RB = sb("RB", [128, 4, 576], BF16)
        PB = [[sb(f"PB{a}{b}", [128, 640], BF16) for b in range(2)] for a in range(2)]
        PTS = [sb(f"PTS{i}", [128, 640], BF16) for i in range(2)]
        DG = [sb(f"DG{i}", [128, 128], BF16) for i in range(4)]
        STAT = sb("STAT", [128, 64], F32)
        ASTAT = sb("ASTAT", [128, 16], F32)
        VG = [sb(f"VG{i}", [128, 512], F32) for i in range(2)]
        VSN = [sb(f"VSN{i}", [128, 512], BF16) for i in range(4)]
        BNS = sb("BNS", [128, 32], F32)
        GATE = [sb(f"GATE{i}", [128, T], F32) for i in range(2)]
        M1 = [sb(f"M1{i}", [128, T], F32) for i in range(2)]
        OB = [sb(f"OB{i}", [128, D], F32) for i in range(2)]
        IDB = sb("IDB", [128, 128], BF16)
        G3 = sb("G3", [128, 24], F32)
        GF = sb("GF", [128, D], F32)
        BG = sb("BG", [128, 16], F32)
        LNG = sb("LNG", [128, 512], F32)
        LNB = sb("LNB", [128, 512], F32)
        WST = sb("WST", [128, 1024], BF16)
        BSH = sb("BSH", [1, 1024], BF16)
        BSF = OB[0][0:1, :]
        BSHF = OB[1][0:1, :]
        BSL = sb("BSL", [1, 1024], BF16)
        ONES = sb("ONES", [1, 128], BF16)
        EPST = sb("EPST", [128, 1], F32)

        p = Prog(nc)

        p.dma("sp", lambda e: e.dma_start(out=G3[:], in_=g3_d), writes=["G3"], sem="c0")
        p.dma("sp", lambda e: e.dma_start(out=GF[:], in_=gfin_d.partition_broadcast(128)), writes=["GF"], sem="c1")
        p.dma("sp", lambda e: e.dma_start(out=BG[:], in_=bgate_d), writes=["BG"], sem="c2")
        p.dma("pool", lambda e: e.dma_start(out=RB[:], in_=relb_d.rearrange("h p k -> p h k")), writes=["RB"], sem="c3")
        p.dma("sp", lambda e: e.dma_start(out=LNG[:], in_=lng_d.partition_broadcast(128)), writes=["LNG"], sem="c4")
        p.dma("sp", lambda e: e.dma_start(out=LNB[:], in_=lnb_d.partition_broadcast(128)), writes=["LNB"], sem="c5")
        p.dma("sp", lambda e: e.dma_start(out=BSF, in_=bs_d), writes=[("OB", 0)], sem="c6")
        p.dma("pool", lambda e: e.dma_start(out=IDB[:], in_=ident_d), writes=["IDB"], sem="c7")
        p.dma("pool", lambda e: e.dma_start(out=WST[:], in_=wst_d), writes=["WST"], sem="c8")
        p.dma("pool", lambda e: e.dma_start(out=BSH[:], in_=bs_d), writes=["BSH"], sem="c9")
        p.add("dve", lambda e: e.memset(EPST[:], EPS), writes=["EPS"])
        p.add("dve", lambda e: e.memset(ONES[:], 1.0), writes=["ONES"])
        p.add("dve", lambda e: e.memset(QBD[:].rearrange("p a b c -> p (a b c)"), 0.0),
              writes=[("QBD", i) for i in range(4)])
        p.add("dve", lambda e: e.memset(VR[:].rearrange("p a b -> p (a b)"), 0.0),
              writes=[("VR", i) for i in range(8)])
        for a in range(2):
            for b in range(2):
                p.add("dve", lambda e, a=a, b=b: e.memset(PB[a][b][:], 0.0), writes=[("PB", a, b)])
        p.add("dve", lambda e: e.memset(WST[:].rearrange("p (g i) -> p g i", g=8)[64:128, :, 0:64], 0.0),
              reads=["WST"], writes=["WST"])
        p.add("dve", lambda e: e.tensor_copy(out=BSHF, in_=BSH[:]), reads=["BSH"], writes=[("OB", 1)])
        p.add("dve", lambda e: e.tensor_tensor(out=BSHF, in0=BSF, in1=BSHF, op=ALU.subtract),
              reads=[("OB", 0), ("OB", 1)], writes=[("OB", 1)])
        p.add("dve", lambda e: e.tensor_copy(out=BSL[:], in_=BSHF), reads=[("OB", 1)], writes=["BSL"])

        slot_ctr = [0]

        converted = set()

        def chunk_aps(chunk):
            kind = chunk[0]
            if kind == "ffs":
                _, i, f2 = chunk
                return (ffs_d[i][2 * f2:2 * f2 + 2].rearrange("f p c -> p f c"),
                        fsrc_d[i][2 * f2:2 * f2 + 2].rearrange("f p c -> p f c"), 4096)
            if kind == "wds":
                _, i, g0, n = chunk
                return (wds_d[i][g0:g0 + n].rearrange("f p c -> p f c"),
                        wd_d[i][g0:g0 + n].rearrange("f p c -> p f c"), n * 1024)
            if kind == "wis":
                return wis_d[chunk[1]], wisrc_d[chunk[1]], 4096
            if kind == "mgs":
                return mgs_d[chunk[1]], mgsrc_d[chunk[1]], 3072
            if kind == "wos":
                return wos_d[chunk[1]], wosrc_d[chunk[1]], 4096
            raise ValueError(chunk)

        def load_slot(chunk):
            i = slot_ctr[0] % NSLOT
            slot_ctr[0] += 1
            scr, src, ncols = chunk_aps(chunk)
            dst = SLOT[i][:, 0:ncols]
            if len(scr.shape) == 3:
                dst = dst.rearrange("p (f c) -> p f c", f=scr.shape[1])
            if chunk not in converted:
                converted.add(chunk)
                p.dma("pool", lambda e: e.dma_start(out=dst, in_=src), writes=[("slot", i)], sem=f"slotq{i}")
                p.dma("sp", lambda e: e.dma_start(out=scr, in_=dst), reads=[("slot", i)], writes=[("SCR", chunk)], sem=f"wb{i}")
            else:
                p.dma("sp", lambda e: e.dma_start(out=dst, in_=scr), reads=[("SCR", chunk)], writes=[("slot", i)], sem=f"slot{i}")
            return i

        def load_x(t):
            par = t % 2
            for s in range(4):
                r0 = t * T + s * 128
                p.dma("act" if t == 0 else "sp",
                      lambda e, par=par, s=s, r0=r0: e.dma_start(out=X[par][:, s, :], in_=x_d[r0:r0 + 128, :]),
                      writes=[("X", par, s, 0), ("X", par, s, 1)], sem=(f"xq{s}" if t == 0 else f"x{par}{s}"))

        ps_tr = [0]
        stat_ctr = [0]

        def emit_norm_stats(t, which, s):
            p.label = "emit_norm_stats(" + ",".join(str(a_) for a_ in (t, which, s,)) + ")"
            par = t % 2
            b = (which * 4 + s) % 4
            c0 = ((which * 4 + s) % 8) * 3
            xres = [("X", par, s, 0), ("X", par, s, 1)]
            p.add("act", lambda e: e.activation(out=XN[b][:], in_=X[par][:, s, :], func=AF.Square,
                                                accum_out=STAT[:, c0:c0 + 1]),
                  reads=xres, writes=[("XN", b), ("st", c0)])
            p.add("act", lambda e: e.activation(out=STAT[:, c0 + 1:c0 + 2], in_=STAT[:, c0:c0 + 1], func=AF.Sqrt,
                                                scale=1.0 / D, bias=EPST[:, 0:1]),
                  reads=[("st", c0), "EPS"], writes=[("st", c0 + 1)])
            p.add("dve", lambda e: e.reciprocal(out=STAT[:, c0 + 2:c0 + 3], in_=STAT[:, c0 + 1:c0 + 2]),
                  reads=[("st", c0 + 1)], writes=[("st", c0 + 2)])
            p.add("act", lambda e: e.activation(out=XN[b][:], in_=X[par][:, s, :], func=AF.Copy,
                                                scale=STAT[:, c0 + 2:c0 + 3]),
                  reads=xres + [("st", c0 + 2)], writes=[("XN", b)])

        def emit_norm_tr(t, which, s):
            p.label = "emit_norm_tr(" + ",".join(str(a_) for a_ in (t, which, s,)) + ")"
            b = (which * 4 + s) % 4
            bi = 6 + (ps_tr[0] % 2)
            ps_tr[0] += 1
            pbf = bank(bi).bitcast(BF16)
            for kc in range(8):
                p.add("pe", lambda e, kc=kc: e.transpose(out=pbf[:, kc * 128:(kc + 1) * 128],
                                                         in_=XN[b][:, kc * 128:(kc + 1) * 128], identity=IDB[:]),
                      reads=[("XN", b), "IDB"], writes=[rb(bi)])
            gb = G3[:, which * 8:(which + 1) * 8].unsqueeze(2).to_broadcast([128, 8, 128])
            p.add("dve", lambda e: e.tensor_tensor(out=XT[:, :, s * 128:(s + 1) * 128],
                                                   in0=pbf.rearrange("p (k c) -> p k c", k=8), in1=gb, op=ALU.mult),
                  reads=[rb(bi), "G3"], writes=[("XT", s)])

        XT_ALL = [("XT", s) for s in range(4)]
        ffn_ps = [0]

        def emit_ffn_stage1(t, i, after_pair=None):
            p.label = "emit_ffn_stage1(" + ",".join(str(a_) for a_ in (t, i,)) + ")"
            for f2 in range(NFC // 2):
                sl = load_slot(("ffs", i, f2))
                for a in range(2):
                    fc = 2 * f2 + a
                    W = SLOT[sl][:, a * 2048:(a + 1) * 2048]
                    k2 = ffn_ps[0] % 2
                    ffn_ps[0] += 1
                    bg, bu = 0 + k2, 2 + k2
                    for kc in range(8):
                        p.add("pe", lambda e, kc=kc, W=W, bg=bg: e.matmul(
                            bank(bg), lhsT=W[:, kc * 128:(kc + 1) * 128], rhs=XT[:, kc, :], start=(kc == 0), stop=(kc == 7)),
                            reads=[("slot", sl)] + XT_ALL, writes=[rb(bg)])
                    for kc in range(8):
                        p.add("pe", lambda e, kc=kc, W=W, bu=bu: e.matmul(
                            bank(bu), lhsT=W[:, 1024 + kc * 128:1024 + (kc + 1) * 128], rhs=XT[:, kc, :],
                            start=(kc == 0), stop=(kc == 7)),
                            reads=[("slot", sl)] + XT_ALL, writes=[rb(bu)])
                    p.add("act", lambda e, k2=k2, bg=bg: e.activation(out=SG[k2][:], in_=bank(bg), func=AF.Silu),
                          reads=[rb(bg)], writes=[("SG", k2)])
                    p.add("dve", lambda e, k2=k2, bu=bu, fc=fc: e.tensor_tensor(out=HT[:, fc, :], in0=bank(bu), in1=SG[k2][:],
                                                                                op=ALU.mult),
                          reads=[rb(bu), ("SG", k2)], writes=[("HT", fc)])
                if after_pair is not None and f2 == 1:
                    after_pair()

        def emit_ffn_stage2_pair(t, i, sp):
            p.label = "emit_ffn_stage2_pair(" + ",".join(str(a_) for a_ in (t, i, sp,)) + ")"
            par = t % 2
            for g0 in range(0, NFC, 4):
                n = min(4, NFC - g0)
                sl = load_slot(("wds", i, g0, n))
                for a in range(n):
                    fc = g0 + a
                    W = SLOT[sl][:, a * 1024:(a + 1) * 1024]
                    for sloc in range(2):
                        s = 2 * sp + sloc
                        for dh in range(2):
                            by = 2 + sloc * 2 + dh
                            p.add("pe", lambda e, fc=fc, W=W, by=by, dh=dh, s=s: e.matmul(
                                bank(by), lhsT=HT[:, fc, s * 128:(s + 1) * 128], rhs=W[:, dh * 512:(dh + 1) * 512],
                                start=(fc == 0), stop=(fc == NFC - 1)),
                                reads=[("slot", sl), ("HT", fc)], writes=[rb(by)])
            for sloc in range(2):
                s = 2 * sp + sloc
                for dh in range(2):
                    by = 2 + sloc * 2 + dh
                    xs = X[par][:, s, dh * 512:(dh + 1) * 512]
                    p.add("dve", lambda e, by=by, xs=xs: e.scalar_tensor_tensor(
                        out=xs, in0=bank(by), scalar=0.5, in1=xs, op0=ALU.mult, op1=ALU.add),
                        reads=[rb(by), ("X", par, s, dh)], writes=[("X", par, s, dh)])

        y_ps = [0]

        mx_ps = [0]

        def next_bank4():
            b = mx_ps[0] % 4
            mx_ps[0] += 1
            return b

        def emit_win(t, after_q_chunk=None):
            p.label = "emit_win(" + ",".join(str(a_) for a_ in (t,)) + ")"
            par = t % 2
            half = t % 2
            for g in (0, 1, 3):
                sl = load_slot(("wis", g))
                W = SLOT[sl]
                for cc in range(4):
                    b = next_bank4()
                    for kc in range(8):
                        p.add("pe", lambda e, kc=kc, W=W, b=b, cc=cc: e.matmul(
                            bank(b), lhsT=W[:, kc * 512 + cc * 128:kc * 512 + (cc + 1) * 128], rhs=XT[:, kc, :],
                            start=(kc == 0), stop=(kc == 7)),
                            reads=[("slot", sl)] + XT_ALL, writes=[rb(b)])
                    if g == 0:
                        for h in range(2):
                            hs = slice(h * 64, (h + 1) * 64)
                            p.add("act", lambda e, b=b, cc=cc, hs=hs: e.activation(
                                out=QBD[hs, cc, :, hs], in_=bank(b)[hs, :].rearrange("p (c q) -> p c q", c=8),
                                func=AF.Copy, scale=0.125),
                                reads=[rb(b)], writes=[("QBD", cc)])
                        if after_q_chunk is not None:
                            after_q_chunk(cc)
                    elif g == 1:
                        p.add("act", lambda e, b=b, cc=cc: e.activation(out=KT[:, cc, half * T:(half + 1) * T], in_=bank(b),
                                                                        func=AF.Copy),
                              reads=[rb(b)], writes=[("KT", cc, half)])
                    else:
                        p.add("act", lambda e, b=b, cc=cc: e.activation(out=UT[:, cc, :], in_=bank(b), func=AF.Gelu_apprx_tanh),
                              reads=[rb(b)], writes=[("UT", cc)])
        def emit_win_v(t, s_list, sl):
            p.label = "emit_win_v(" + ",".join(str(a_) for a_ in (t, s_list, sl,)) + ")"
            half = t % 2
            W = SLOT[sl]
            for s in s_list:
                b = next_bank4()
                for kc in range(8):
                    p.add("pe", lambda e, kc=kc, W=W, b=b, s=s: e.matmul(
                        bank(b), lhsT=XT[:, kc, s * 128:(s + 1) * 128], rhs=W[:, kc * 512:(kc + 1) * 512],
                        start=(kc == 0), stop=(kc == 7)),
                        reads=[("slot", sl), ("XT", s)], writes=[rb(b)])
                vs_ = half * 4 + s
                p.add("dve", lambda e, b=b, vs_=vs_: e.tensor_copy(out=VR[:, vs_, :], in_=bank(b)),
                      reads=[rb(b)], writes=[("VR", vs_)])

        VG_BUF = [(VG[0][:], ("VG", 0)), (VG[1][:], ("VG", 1)), (GATE[0][:], ("GATE", 0)), (GATE[1][:], ("GATE", 1))]

        def emit_sgu_front(t, s_list, sl):
            p.label = "emit_sgu_front(" + ",".join(str(a_) for a_ in (t, s_list, sl,)) + ")"
            W = SLOT[sl]
            for s in s_list:
                b = next_bank4()
                vg, vgres = VG_BUF[s]
                for kc in range(8):
                    p.add("pe", lambda e, kc=kc, b=b, s=s: e.matmul(
                        bank(b), lhsT=XT[:, kc, s * 128:(s + 1) * 128], rhs=W[:, kc * 512:(kc + 1) * 512],
                        start=(kc == 0), stop=(kc == 7)),
                        reads=[("slot", sl), ("XT", s)], writes=[rb(b)])
                p.add("act", lambda e, b=b, vg=vg: e.activation(out=vg, in_=bank(b), func=AF.Gelu_apprx_tanh),
                      reads=[rb(b)], writes=[vgres])

        def emit_sgu_ln(t, s_list):
            for s in s_list:
                vg, vgres = VG_BUF[s]
                b0 = s * 8
                p.add("dve", lambda e, vg=vg, b0=b0: e.bn_stats(out=BNS[:, b0:b0 + 6], in_=vg), reads=[vgres], writes=[("BNS", s)])
                p.add("dve", lambda e, b0=b0: e.bn_aggr(out=BNS[:, b0 + 6:b0 + 8], in_=BNS[:, b0:b0 + 6]),
                      reads=[("BNS", s)], writes=[("BNMV", s)])
                c0 = 48 + s * 2
                p.add("act", lambda e, c0=c0, b0=b0: e.activation(out=STAT[:, c0:c0 + 1], in_=BNS[:, b0 + 7:b0 + 8], func=AF.Sqrt,
                                                                  scale=1.0, bias=EPST[:, 0:1]),
                      reads=[("BNMV", s), "EPS"], writes=[("st", c0)])
                p.add("dve", lambda e, c0=c0: e.reciprocal(out=STAT[:, c0 + 1:c0 + 2], in_=STAT[:, c0:c0 + 1]),
                      reads=[("st", c0)], writes=[("st", c0 + 1)])
                p.add("dve", lambda e, vg=vg, c0=c0, b0=b0: e.tensor_scalar(
                    out=vg, in0=vg, scalar1=BNS[:, b0 + 6:b0 + 7], scalar2=STAT[:, c0 + 1:c0 + 2],
                    op0=ALU.subtract, op1=ALU.mult),
                    reads=[vgres, ("BNMV", s), ("st", c0 + 1)], writes=[vgres])
                p.add("dve", lambda e, vg=vg: e.tensor_tensor(out=vg, in0=vg, in1=LNG[:], op=ALU.mult),
                      reads=[vgres, "LNG"], writes=[vgres])
                p.add("dve", lambda e, vg=vg, s=s: e.tensor_tensor(out=VSN[s][:], in0=vg, in1=LNB[:], op=ALU.add),
                      reads=[vgres, "LNB"], writes=[("VSN", s)])

        def emit_sgu_mix(t, s_list=(0, 1, 2, 3)):
            p.label = "emit_sgu_mix(" + ",".join(str(a_) for a_ in (t,)) + ")"
            bb = (4, 5)
            SM = PS[2]
            for s in s_list:
                for pr in range(4):
                    o = SM[:, pr * 256:(pr + 1) * 256]
                    rbk = rb(bb[pr // 2])
                    p.add("pe", lambda e, o=o, pr=pr: e.matmul(o, lhsT=ONES[0:1, :], rhs=BSH[0:1, pr * 256:(pr + 1) * 256],
                                                              start=True, stop=False),
                          reads=["ONES", "BSH"], writes=[rbk])
                    p.add("pe", lambda e, o=o, pr=pr: e.matmul(o, lhsT=ONES[0:1, :], rhs=BSL[0:1, pr * 256:(pr + 1) * 256],
                                                              start=False, stop=False),
                          reads=["ONES", "BSL"], writes=[rbk])
                    p.add("pe", lambda e, o=o, pr=pr, s=s: e.matmul(o, lhsT=VSN[s][:, pr * 128:(pr + 1) * 128],
                                                                   rhs=WST[:, pr * 256:(pr + 1) * 256],
                                                                   start=False, stop=True),
                          reads=[("VSN", s), "WST"], writes=[rbk])
                for pr in range(4):
                    rbk = rb(bb[pr // 2])
                    for h in range(2):
                        hs = slice(h * 64, (h + 1) * 64)
                        p.add("dve", lambda e, pr=pr, h=h, hs=hs, s=s: e.tensor_tensor(
                            out=HT[hs, 12 + pr, s * 128:(s + 1) * 128],
                            in0=SM[hs, pr * 256 + h * 128:pr * 256 + (h + 1) * 128],
                            in1=UT[hs, pr, s * 128:(s + 1) * 128], op=ALU.mult),
                            reads=[rbk, ("UT", pr)], writes=[("HT", 12 + pr)])

        def emit_attention(t, extra=None):
            p.label = "emit_attention(" + ",".join(str(a_) for a_ in (t,)) + ")"
            half = t % 2
            items = [(hp, cl) for hp in range(4) for cl in range(8)]
            N = len(items)

            def geom(i):
                hp, cl = items[i]
                c = t * 8 + cl
                n_prev = 0 if t == 0 else 8 - cl
                n_cur = cl + 1
                nb = (n_prev + n_cur) * 64
                return dict(hp=hp, cl=cl, n_prev=n_prev, n_cur=n_cur, j0=576 - nb, off=64 * (cl % 2),
                            gb0=(c - 8) // 2, k2=i % 2, k4=i % 4, a0=(i % 4) * 4,
                            pbuf=PB[cl % 2][(i // 2) % 2], pres=("PB", cl % 2, (i // 2) % 2))

            def stage_scores(i):
                g = geom(i)
                hp, cl, n_prev, n_cur, j0, k2, a0 = g["hp"], g["cl"], g["n_prev"], g["n_cur"], g["j0"], g["k2"], g["a0"]
                S2 = PS[k2]
                sres = [rb(2 * k2), rb(2 * k2 + 1)]
                if n_prev:
                    o = S2[:, 512 - n_prev * 64:512]
                    p.add("pe", lambda e: e.matmul(
                        o, lhsT=QBD[:, hp, cl, :],
                        rhs=KT[:, hp, (1 - half) * T + cl * 64:(1 - half) * T + T], start=True, stop=False),
                        reads=[("QBD", hp), ("KT", hp, 1 - half)], writes=[sres[0]])
                    p.add("pe", lambda e: e.matmul(o, lhsT=IDB[:], rhs=RB[:, hp, 0:n_prev * 64], start=False, stop=True),
                          reads=["IDB", "RB"], writes=[sres[0]])
                o2 = S2[:, 512:512 + n_cur * 64]
                p.add("pe", lambda e: e.matmul(
                    o2, lhsT=QBD[:, hp, cl, :],
                    rhs=KT[:, hp, half * T:half * T + n_cur * 64], start=True, stop=False),
                    reads=[("QBD", hp), ("KT", hp, half)], writes=[sres[1]])
                p.add("pe", lambda e: e.matmul(o2, lhsT=IDB[:], rhs=RB[:, hp, 576 - n_cur * 64:576], start=False, stop=True),
                      reads=["IDB", "RB"], writes=[sres[1]])
                band = S2[:, 512 - n_prev * 64:512 + n_cur * 64]
                p.add("dve", lambda e: e.reduce_max(out=ASTAT[:, a0:a0 + 1], in_=band, axis=AX.X, negate=True),
                      reads=sres, writes=[("as", a0)])

            def stage_exp(i):
                g = geom(i)
                j0, k2, a0, off, pbuf, pres, n_prev, n_cur = g["j0"], g["k2"], g["a0"], g["off"], g["pbuf"], g["pres"], g["n_prev"], g["n_cur"]
                S2 = PS[k2]
                sres = [rb(2 * k2), rb(2 * k2 + 1)]
                band = S2[:, 512 - n_prev * 64:512 + n_cur * 64]
                if t == 0 and j0 > 0:
                    p.add("dve", lambda e: e.memset(pbuf[:, off:off + j0], 0.0), writes=[pres])
                p.add("act", lambda e: e.activation(
                    out=pbuf[:, off + j0:off + 576], in_=band, func=AF.Exp,
                    bias=ASTAT[:, a0:a0 + 1], scale=1.0, accum_out=ASTAT[:, a0 + 1:a0 + 2]),
                    reads=sres + [("as", a0)], writes=[pres, ("as", a0 + 1)])

            def stage_norm(i):
                g = geom(i)
                k4, a0 = g["k4"], g["a0"]
                p.add("dve", lambda e: e.reciprocal(out=ASTAT[:, a0 + 2:a0 + 3], in_=ASTAT[:, a0 + 1:a0 + 2]),
                      reads=[("as", a0 + 1)], writes=[("as", a0 + 2)])
                p.add("act", lambda e: e.activation(out=DG[k4][:], in_=IDB[:], func=AF.Copy, scale=ASTAT[:, a0 + 2:a0 + 3]),
                      reads=["IDB", ("as", a0 + 2)], writes=[("DG", k4)])

            def stage_pt(i):
                g = geom(i)
                k2, k4, pbuf, pres = g["k2"], g["k4"], g["pbuf"], g["pres"]
                PTp = PS[2]
                for kt in range(5):
                    pc = kt * 128 if kt < 3 else 512 + (kt - 3) * 128
                    p.add("pe", lambda e, kt=kt, pc=pc: e.matmul(
                        PTp[:, pc:pc + 128], lhsT=pbuf[:, kt * 128:(kt + 1) * 128], rhs=DG[k4][:],
                        start=True, stop=True),
                        reads=[pres, ("DG", k4)], writes=[rb(4 + kt // 3)])
                p.add("act", lambda e: e.activation(out=PTS[k2][:, 0:384], in_=PTp[:, 0:384], func=AF.Copy),
                      reads=[rb(4)], writes=[("PTS", k2, 0)])
                p.add("dve", lambda e: e.tensor_copy(out=PTS[k2][:, 384:640], in_=PTp[:, 512:768]),
                      reads=[rb(5)], writes=[("PTS", k2, 1)])

            def stage_pv(i):
                g = geom(i)
                hp, cl, k2, gb0 = g["hp"], g["cl"], g["k2"], g["gb0"]
                ob = 6 + k2
                for kt in range(5):
                    vslot = (gb0 + kt) % 8
                    p.add("pe", lambda e, kt=kt, vslot=vslot: e.matmul(
                        bank(ob)[:, 0:128], lhsT=VR[:, vslot, hp * 128:(hp + 1) * 128],
                        rhs=PTS[k2][:, kt * 128:(kt + 1) * 128], start=(kt == 0), stop=(kt == 4)),
                        reads=[("VR", vslot), ("PTS", k2, kt // 3)], writes=[rb(ob)])
                for h in range(2):
                    hs = slice(h * 64, (h + 1) * 64)
                    p.add("dve", lambda e, hs=hs, h=h: e.tensor_copy(
                        out=HT[hs, 8 + hp, cl * 64:(cl + 1) * 64], in_=bank(ob)[hs, h * 64:(h + 1) * 64]),
                        reads=[rb(ob)], writes=[("HT", 8 + hp)])

            for step in range(N + 4):
                if extra is not None and step in extra:
                    extra[step]()
                    p.label = "emit_attention(" + str(t) + ")"
                if step < N:
                    stage_scores(step)
                if 0 <= step - 3 < N:
                    stage_pt(step - 3)
                if step < N:
                    stage_exp(step)
                if 0 <= step - 1 < N:
                    stage_norm(step - 1)
                if 0 <= step - 4 < N:
                    stage_pv(step - 4)

        merge_pre = {}

        def emit_merge_gates(t, dc):
            sl = load_slot(("mgs", dc))
            W = SLOT[sl]
            b0 = 4 * (dc % 2)
            for j in range(2):
                bgt = b0 + 2 * j
                for kc in range(8):
                    p.add("pe", lambda e, kc=kc, W=W, bgt=bgt, j=j: e.matmul(
                        bank(bgt), lhsT=W[:, j * 1024 + kc * 128:j * 1024 + (kc + 1) * 128], rhs=XT[:, kc, :],
                        start=(kc == 0), stop=(kc == 7)),
                        reads=[("slot", sl)] + XT_ALL, writes=[rb(bgt)])
            merge_pre[(t, dc)] = sl

        def emit_merge(t):
            p.label = "emit_merge(" + ",".join(str(a_) for a_ in (t,)) + ")"
            par = t % 2
            YA = [("HT", 8 + i) for i in range(4)]
            YS = [("HT", 12 + i) for i in range(4)]
            for dc in range(8):
                if (t, dc) not in merge_pre:
                    emit_merge_gates(t, dc)
                sl = merge_pre[(t, dc)]
                W = SLOT[sl]
                k2 = dc % 2
                b0 = 4 * k2
                for j in range(2):
                    bgt, bbr = b0 + 2 * j, b0 + 2 * j + 1
                    for c4 in range(4):
                        p.add("pe", lambda e, c4=c4, W=W, bbr=bbr, j=j: e.matmul(
                            bank(bbr), lhsT=W[:, 2048 + j * 512 + c4 * 128:2048 + j * 512 + (c4 + 1) * 128],
                            rhs=HT[:, 8 + 4 * j + c4, :], start=(c4 == 0), stop=(c4 == 3)),
                            reads=[("slot", sl)] + (YA if j == 0 else YS), writes=[rb(bbr)])
                    p.add("act", lambda e, bgt=bgt, j=j, dc=dc: e.activation(
                        out=GATE[j][:], in_=bank(bgt), func=AF.Sigmoid, bias=BG[:, j * 8 + dc:j * 8 + dc + 1], scale=1.0),
                        reads=[rb(bgt), "BG"], writes=[("GATE", j)])
                    p.add("dve", lambda e, bbr=bbr, j=j: e.tensor_tensor(out=M1[j][:], in0=bank(bbr), in1=GATE[j][:], op=ALU.mult),
                          reads=[rb(bbr), ("GATE", j)], writes=[("M1", j)])
                p.add("dve", lambda e, dc=dc: e.tensor_tensor(out=HT[:, dc, :], in0=M1[0][:], in1=M1[1][:], op=ALU.add),
                      reads=[("M1", 0), ("M1", 1)], writes=[("HT", dc)])
            sls = [load_slot(("wos", dh)) for dh in range(2)]
            for s in range(4):
                for dh in range(2):
                    W = SLOT[sls[dh]]
                    by = 4 + (y_ps[0] % 2)
                    y_ps[0] += 1
                    for dc in range(8):
                        p.add("pe", lambda e, dc=dc, W=W, by=by, s=s: e.matmul(
                            bank(by), lhsT=HT[:, dc, s * 128:(s + 1) * 128], rhs=W[:, dc * 512:(dc + 1) * 512],
                            start=(dc == 0), stop=(dc == 7)),
                            reads=[("slot", sls[dh]), ("HT", dc)], writes=[rb(by)])
                    xs = X[par][:, s, dh * 512:(dh + 1) * 512]
                    p.add("dve", lambda e, by=by, xs=xs: e.tensor_tensor(out=xs, in0=bank(by), in1=xs, op=ALU.add),
                          reads=[rb(by), ("X", par, s, dh)], writes=[("X", par, s, dh)])
                emit_norm_stats(t, 2, s)
                if s >= 2:
                    emit_norm_tr(t, 2, s - 2)
            emit_norm_tr(t, 2, 2)
            emit_norm_tr(t, 2, 3)

        pending_stores = []

        def flush_stores():
            while pending_stores:
                pending_stores.pop(0)()

        def emit_final(t, s, defer=False):
            p.label = "emit_final(" + ",".join(str(a_) for a_ in (t, s,)) + ")"
            par = t % 2
            b = s % 2
            c0 = 24 + b * 3
            xres = [("X", par, s, 0), ("X", par, s, 1)]
            p.add("act", lambda e: e.activation(out=OB[b][:], in_=X[par][:, s, :], func=AF.Square,
                                                accum_out=STAT[:, c0:c0 + 1]),
                  reads=xres, writes=[("OB", b), ("st", c0)])
            p.add("act", lambda e: e.activation(out=STAT[:, c0 + 1:c0 + 2], in_=STAT[:, c0:c0 + 1], func=AF.Sqrt,
                                                scale=1.0 / D, bias=EPST[:, 0:1]),
                  reads=[("st", c0), "EPS"], writes=[("st", c0 + 1)])
            p.add("dve", lambda e: e.reciprocal(out=STAT[:, c0 + 2:c0 + 3], in_=STAT[:, c0 + 1:c0 + 2]),
                  reads=[("st", c0 + 1)], writes=[("st", c0 + 2)])
            p.add("dve", lambda e: e.scalar_tensor_tensor(out=OB[b][:], in0=X[par][:, s, :], scalar=STAT[:, c0 + 2:c0 + 3],
                                                          in1=GF[:], op0=ALU.mult, op1=ALU.mult),
                  reads=xres + [("st", c0 + 2), "GF"], writes=[("OB", b)])
            r0 = t * T + s * 128

            def store():
                p.dma("act", lambda e: e.dma_start(out=out_d[r0:r0 + 128, :], in_=OB[b][:]),
                      reads=[("OB", b)], writes=[("outd", b)], sem=f"o{b}")
            if defer:
                pending_stores.append(store)
            else:
                store()

        load_x(0)
        for s in range(4):
            emit_norm_stats(0, 0, s)
            emit_norm_tr(0, 0, s)
        for t in range(n_tiles):
            emit_ffn_stage1(t, 0, after_pair=flush_stores)
            if t + 1 < n_tiles:
                load_x(t + 1)
            emit_ffn_stage2_pair(t, 0, 0)
            emit_norm_stats(t, 1, 0)
            emit_norm_stats(t, 1, 1)
            emit_ffn_stage2_pair(t, 0, 1)
            emit_norm_tr(t, 1, 0)
            emit_norm_tr(t, 1, 1)
            emit_norm_stats(t, 1, 2)
            emit_norm_stats(t, 1, 3)
            slv = load_slot(("wis", 2))
            slvs = load_slot(("wis", 4))
            emit_win_v(t, [0, 1], slv)
            emit_sgu_front(t, [0, 1], slvs)
            emit_norm_tr(t, 1, 2)
            emit_norm_tr(t, 1, 3)
            emit_win_v(t, [2, 3], slv)
            emit_sgu_front(t, [2, 3], slvs)
            emit_win(t, after_q_chunk=lambda cc, t=t: emit_sgu_ln(t, [cc]))
            emit_sgu_mix(t, (0, 1))
            emit_attention(t, extra={1: (lambda t=t: emit_sgu_mix(t, (2,))), 2: (lambda t=t: emit_sgu_mix(t, (3,))),
                                     32: (lambda t=t: emit_merge_gates(t, 0))})
            emit_merge(t)
            if debug and t == n_tiles - 1:
                p.dma("pool", lambda e: e.dma_start(out=dbg_d, in_=HT[:, 0:16, :]),
                      reads=[("HT", i) for i in range(16)], writes=["dbg"], sem="dbg")
            emit_ffn_stage1(t, 1)
            if t + 1 < n_tiles:
                for s in range(4):
                    emit_norm_stats(t + 1, 0, s)
            emit_ffn_stage2_pair(t, 1, 0)
            if t + 1 < n_tiles:
                for s in range(4):
                    emit_norm_tr(t + 1, 0, s)
            emit_final(t, 0)
            emit_final(t, 1)
            emit_ffn_stage2_pair(t, 1, 1)
            emit_final(t, 2, defer=(t + 1 < n_tiles))
            emit_final(t, 3, defer=(t + 1 < n_tiles))
        p.add("sp", lambda e: e.nop(), reads=[("outd", 0), ("outd", 1)] + (["dbg"] if debug else []))
        _NC_CACHE['prog'] = p
        p.emit()
        print('sbuf bytes remaining', nc.sbuf_bytes_remaining)
    return nc


_NC_CACHE = {}


def _host_inputs(inputs):
    f = lambda a: np.ascontiguousarray(np.asarray(a, dtype=np.float32))
    g3 = np.concatenate([f(inputs[k])[0].reshape(8, 128).T for k in ("norm_ffn1", "norm_mix", "norm_ffn2")], axis=1)
    rel_tab = f(inputs["rel_bias"])[0]
    qi = np.arange(64)[:, None]
    kj = np.arange(576)[None, :]
    rel = np.clip(qi + 512 - kj, -256, 256) + 256
    relb = rel_tab[:, rel].reshape(4, 128, 576)
    def chunked(w, nk, nc_, cw):
        return w.reshape(nk, 128, nc_, cw).transpose(2, 1, 0, 3).reshape(nc_, 128, nk * cw)

    def fsrc(wg, wu):
        return f(np.concatenate([chunked(f(wg)[0], 8, NFC, 128), chunked(f(wu)[0], 8, NFC, 128)], axis=2))

    w_in = f(inputs["w_in"])[0]
    mgsrc = np.concatenate([chunked(w_in[:, 2560:3584], 8, 8, 128), chunked(w_in[:, 3584:4608], 8, 8, 128),
                            chunked(f(inputs["w_branch_att"])[0], 4, 8, 128),
                            chunked(f(inputs["w_branch_sgu"])[0], 4, 8, 128)], axis=2)
    shared = {
        "fsrc1": fsrc(inputs["ffn1_w_gate"], inputs["ffn1_w_up"]),
        "fsrc2": fsrc(inputs["ffn2_w_gate"], inputs["ffn2_w_up"]),
        "wd1": f(f(inputs["ffn1_w_down"])[0].reshape(NFC, 128, 1024)),
        "wd2": f(f(inputs["ffn2_w_down"])[0].reshape(NFC, 128, 1024)),
        "wisrc": f(chunked(w_in[:, 0:2560], 8, 5, 512)),
        "mgsrc": f(mgsrc),
        "wosrc": f(chunked(f(inputs["w_out"])[0], 8, 2, 512)),
        "g3": f(g3), "gfin": f(inputs["norm_final"]),
        "bgate": f(f(inputs["b_gate"])[0].reshape(16, 128).T),
        "relb": f(relb),
        "lng": f(inputs["sgu_ln_g"])[0], "lnb": f(inputs["sgu_ln_b"])[0],
        "wst": f(f(inputs["sgu_w_s"])[0].transpose(2, 0, 1).reshape(128, 1024)),
        "bs": f(f(inputs["sgu_b_s"])[0].reshape(1, 1024)),
        "ident": np.eye(128, dtype=np.float32),
    }
    return shared


def kernel(**inputs):
    x = np.asarray(inputs["x"], dtype=np.float32)
    shared = _host_inputs(inputs)
    if "nc" not in _NC_CACHE:
        _NC_CACHE["nc"] = build_nc()
    nc = _NC_CACHE["nc"]
    in_maps = []
    for b in range(8):
        m = dict(shared)
        m["x"] = np.ascontiguousarray(x[b])
        in_maps.append(m)
    res = run_bass_kernel_spmd(nc, in_maps, core_ids=list(range(8)))
    return np.stack([np.asarray(r["out"], dtype=np.float32) for r in res.results], axis=0)
```
